# Optimizing a Trainium2 kernel written in Bass

```python
import jax, jax.numpy as jnp
from jax import lax
import numpy as np

D_MODEL = 1024
BATCH = 8
SEQ = 2048
DEPTH = 2
DEC_BATCH = 128
DEC_SEQ = 4
PAST_LEN = 16384
PAGE_SIZE = 128

EXPAND = 2
D_MIX = EXPAND * D_MODEL
D_A = D_MIX // 2
D_B = D_MIX - D_A
N_HEADS_A = 8
HEAD_DIM_A = D_A // N_HEADS_A
N_HEADS_B = 8
CONV_A_W = 31
CONV_B_W = 3
D_IN = 3 * D_A + 4 * D_B
RMS_EPS = 1e-6
LN_EPS = 1e-5

kernel_name = "hybrid_conformer_shortconv_decode_step"


def rmsnorm(x, g):
    xf = x.astype(jnp.float32)
    xn = xf * lax.rsqrt(jnp.mean(xf * xf, axis=-1, keepdims=True) + RMS_EPS)
    return xn.astype(x.dtype) * g


def head_layernorm(x, g, b):
    bsz, t, c = x.shape
    xf = x.astype(jnp.float32).reshape(bsz, t, N_HEADS_A, HEAD_DIM_A)
    mu = jnp.mean(xf, axis=-1, keepdims=True)
    var = jnp.mean(jnp.square(xf - mu), axis=-1, keepdims=True)
    xn = ((xf - mu) * lax.rsqrt(var + LN_EPS)).reshape(bsz, t, c)
    return xn.astype(x.dtype) * g + b


def causal_dwconv(x, buf, w):
    width, c = w.shape
    xp = jnp.concatenate([buf.astype(x.dtype), x], axis=1)
    y = lax.conv_general_dilated(
        xp, w.reshape(width, 1, c).astype(x.dtype),
        window_strides=(1,), padding='VALID',
        dimension_numbers=('NWC', 'WIO', 'NWC'),
        feature_group_count=c)
    new_buf = xp[:, xp.shape[1] - (width - 1):, :]
    return y, new_buf


def hybrid_layer(x, buf_a, buf_b, norm_g, w_in, conv_a_w, conv_a_b, ln_a_g, ln_a_b, conv_b_w, w_out):
    h = rmsnorm(x, norm_g)
    p = jnp.einsum('btd,de->bte', h, w_in)
    splits = [D_A, 2 * D_A, 3 * D_A, 3 * D_A + D_B, 3 * D_A + 2 * D_B, 3 * D_A + 3 * D_B]
    a_val, a_gate, z_a, gb, gc, hb, z_b = jnp.split(p, splits, axis=-1)
    u = a_val * jax.nn.sigmoid(a_gate)
    ca, new_a = causal_dwconv(u, buf_a, conv_a_w)
    ca = head_layernorm(ca + conv_a_b, ln_a_g, ln_a_b)
    y_a = jax.nn.silu(ca) * jax.nn.silu(z_a)
    v = gc * hb
    cb, new_b = causal_dwconv(v, buf_b, conv_b_w)
    y_b = gb * cb * jax.nn.silu(z_b)
    y = jnp.einsum('bte,ed->btd', jnp.concatenate([y_a, y_b], axis=-1), w_out)
    return x + y, new_a, new_b


def trunk(x, bufs_a, bufs_b, norm_g, w_in, conv_a_w, conv_a_b, ln_a_g, ln_a_b, conv_b_w, w_out, final_g):
    new_as, new_bs = [], []
    for l in range(DEPTH):
        x, na, nb = hybrid_layer(x, bufs_a[l], bufs_b[l], norm_g[l], w_in[l], conv_a_w[l], conv_a_b[l],
                                 ln_a_g[l], ln_a_b[l], conv_b_w[l], w_out[l])
        new_as.append(na)
        new_bs.append(nb)
    return rmsnorm(x, final_g), jnp.stack(new_as, axis=0), jnp.stack(new_bs, axis=0)


def setup_inputs(seed: int = 0) -> dict:
    key = jax.random.key(seed)
    ks = jax.random.split(key, 14)
    f32 = jnp.float32
    x_prompt = jax.random.normal(ks[0], (BATCH, SEQ, D_MODEL), f32)
    x_sample = jax.random.normal(ks[1], (DEC_BATCH, DEC_SEQ, D_MODEL), f32)
    state_conv_a = 0.5 * jax.random.normal(ks[2], (DEPTH, DEC_BATCH, CONV_A_W - 1, D_A), f32)
    state_conv_b = 0.5 * jax.random.normal(ks[3], (DEPTH, DEC_BATCH, CONV_B_W - 1, D_B), f32)
    norm_g = 1.0 + 0.02 * jax.random.normal(ks[4], (DEPTH, D_MODEL), f32)
    w_in = jax.random.normal(ks[5], (DEPTH, D_MODEL, D_IN), f32) * D_MODEL ** -0.5
    conv_a_w = jax.random.normal(ks[6], (DEPTH, CONV_A_W, D_A), f32) * CONV_A_W ** -0.5
    conv_a_b = 0.01 * jax.random.normal(ks[7], (DEPTH, D_A), f32)
    ln_a_g = 1.0 + 0.02 * jax.random.normal(ks[8], (DEPTH, D_A), f32)
    ln_a_b = 0.01 * jax.random.normal(ks[9], (DEPTH, D_A), f32)
    conv_b_w = jax.random.normal(ks[10], (DEPTH, CONV_B_W, D_B), f32) * CONV_B_W ** -0.5
    w_out = jax.random.normal(ks[11], (DEPTH, D_MIX, D_MODEL), f32) * D_MIX ** -0.5
    final_g = 1.0 + 0.02 * jax.random.normal(ks[12], (D_MODEL,), f32)
    return {"x_prompt": x_prompt, "x_sample": x_sample,
            "state_conv_a": state_conv_a, "state_conv_b": state_conv_b,
            "norm_g": norm_g, "w_in": w_in, "conv_a_w": conv_a_w, "conv_a_b": conv_a_b,
            "ln_a_g": ln_a_g, "ln_a_b": ln_a_b, "conv_b_w": conv_b_w, "w_out": w_out,
            "final_g": final_g}


def reference(x_prompt, x_sample, state_conv_a, state_conv_b, norm_g, w_in, conv_a_w, conv_a_b,
              ln_a_g, ln_a_b, conv_b_w, w_out, final_g):
    bp = x_prompt.shape[0]
    zeros_a = jnp.zeros((DEPTH, bp, CONV_A_W - 1, D_A), x_prompt.dtype)
    zeros_b = jnp.zeros((DEPTH, bp, CONV_B_W - 1, D_B), x_prompt.dtype)
    y_prompt, new_conv_a_prompt, new_conv_b_prompt = trunk(
        x_prompt, zeros_a, zeros_b, norm_g, w_in, conv_a_w, conv_a_b, ln_a_g, ln_a_b, conv_b_w, w_out, final_g)
    y_sample, new_conv_a_sample, new_conv_b_sample = trunk(
        x_sample, state_conv_a, state_conv_b, norm_g, w_in, conv_a_w, conv_a_b, ln_a_g, ln_a_b, conv_b_w, w_out, final_g)
    return (y_prompt, y_sample, new_conv_a_prompt, new_conv_b_prompt, new_conv_a_sample, new_conv_b_sample)
```

```python
from contextlib import ExitStack
import numpy as np
import concourse.bass as bass
import concourse.mybir as mybir
from concourse.bass_utils import run_bass_kernel_spmd

F32 = mybir.dt.float32
BF16 = mybir.dt.bfloat16
AF = mybir.ActivationFunctionType
ALU = mybir.AluOpType

N_CORES = 8
D = 1024
DEPTH = 2
SEQ = 2048
NS = 16
DS = 4
T = SEQ + NS * DS
KC = 8
WA = 31
WB = 3
RMS_EPS = 1e-6
LN_EPS = 1e-5
BLOCKS = [(0, 512), (512, 512), (1024, 512), (1536, 512), (2048, 64)]
NB = len(BLOCKS)
NW = 9
UH = 30
US = UH + SEQ
UW = US + NS * (WA - 1 + DS)
VH = 2
VS = VH + SEQ
VW = VS + NS * (WB - 1 + DS)
K_SAME = 3
DRAIN_DELAY = 6
RELAX_INPLACE = False
N_DVE = 12
N_DVE_PRE_SAMPLE = 12


class Res:
    __slots__ = ("name", "w", "r")

    def __init__(self, name):
        self.name = name
        self.w = None
        self.r = []


class Op:
    __slots__ = ("eng", "fn", "deps", "idx", "sig", "cnt", "chan", "name")


class Prog:
    ENGS = ("pe", "act", "dve", "pool", "sp")

    def __init__(self):
        self.ops = {e: [] for e in self.ENGS}
        self.chan_ops = {}

    def op(self, eng, fn, reads=(), writes=(), chan=None, name="", relaxed=()):
        if not RELAX_INPLACE:
            relaxed = ()
        o = Op()
        o.eng, o.fn, o.chan, o.name = eng, fn, chan, name
        o.sig = False
        o.cnt = 0
        o.idx = len(self.ops[eng])
        deps = set()
        for r in reads:
            if r.w is not None and not (r in relaxed and r.w.eng == eng and r.w.chan is None):
                deps.add(r.w)
        for w in writes:
            rel = w in relaxed
            if w.w is not None and not (rel and w.w.eng == eng and w.w.chan is None):
                deps.add(w.w)
            for x in w.r:
                if not (rel and x.eng == eng and x.chan is None):
                    deps.add(x)
        deps.discard(o)
        keep = []
        for d in deps:
            if d.chan is None and chan is None and d.eng == eng:
                if eng == "pe":
                    continue
                if o.idx - d.idx > K_SAME:
                    continue
            keep.append(d)
            d.sig = True
        o.deps = keep
        for r in reads:
            r.r.append(o)
        for w in writes:
            w.w = o
            w.r = []
        self.ops[eng].append(o)
        if chan is not None:
            self.chan_ops.setdefault(chan, []).append(o)
        return o

    def emit(self, nc, block, sems, chan_sems, final_waits):
        for e in self.ENGS:
            c = 0
            for o in self.ops[e]:
                if o.chan is None and o.sig:
                    c += 1
                    o.cnt = c
        for ch, lst in self.chan_ops.items():
            c = 0
            for o in lst:
                c += 16
                o.cnt = c
        chan_total = {ch: 16 * len(lst) for ch, lst in self.chan_ops.items()}

        def run(eng_name):
            def body(e):
                waited = {}
                for o in self.ops[eng_name]:
                    need = {}
                    for d in o.deps:
                        s = chan_sems[d.chan] if d.chan is not None else sems[d.eng]
                        if need.get(s.num, (None, 0))[1] < d.cnt:
                            need[s.num] = (s, d.cnt)
                    for num, (s, c) in need.items():
                        if waited.get(num, 0) < c:
                            e.wait_ge(s, c)
                            waited[num] = c
                    ins = o.fn(e)
                    if o.chan is not None:
                        ins.then_inc(chan_sems[o.chan], 16)
                    elif o.sig:
                        ins.then_inc(sems[eng_name], 1)
                if eng_name == "sp":
                    for ch in final_waits:
                        e.wait_ge(chan_sems[ch], chan_total[ch])
            return body

        block.tensor(run("pe"))
        block.scalar(run("act"))
        block.vector(run("dve"))
        block.gpsimd(run("pool"))
        block.sync(run("sp"))


DEBUG = False


def build_nc():
    nc = bass.Bass("TRN2", target_bir_lowering=False)
    P = Prog()
    es = ExitStack()

    def din(name, shape, dt=F32):
        return nc.dram_tensor(name, list(shape), dt, kind="ExternalInput").ap()

    def dout(name, shape, dt=F32):
        return nc.dram_tensor(name, list(shape), dt, kind="ExternalOutput").ap()

    xT = din("xT", [128, KC, T])
    w_in_r = din("w_in_r", [DEPTH, 56, 128, 1024])
    w_out_r = din("w_out_r", [DEPTH, 16, 128, 1024])
    cw_a_d = din("cw_a", [128, DEPTH, 8, WA])
    cw_b_d = din("cw_b", [128, DEPTH, 8, WB])
    vecs_d = din("vecs", [128, 72])
    hist_a_d = din("hist_a", [128, DEPTH, 8, (WA - 1) * NS])
    hist_b_d = din("hist_b", [128, DEPTH, 8, (WB - 1) * NS])
    sa_orig = din("sa_orig", [DEPTH, NS, WA - 1, D])
    cmat_d = din("cmat", [128, 2, 128])

    y_fm = dout("y_fm", [128, KC, T])
    oa_p = dout("oa_p", [128, DEPTH, 8, WA - 1])
    ob_p = dout("ob_p", [128, DEPTH, 8, WB - 1])
    oa_copy = dout("oa_copy", [DEPTH, NS, WA - 1 - DS, D])
    oa_new = dout("oa_new", [128, DEPTH, 8, NS * DS])
    ob_new = dout("ob_new", [128, DEPTH, 8, NS * DS])

    if DEBUG:
        dbg_y = dout("dbg_y", [128, DEPTH, 16, NS * DS], BF16)
        dbg_x = dout("dbg_x", [128, DEPTH, KC, NS * DS], F32)

    def sb(name, shape, dt):
        return es.enter_context(nc.sbuf_tensor(name, list(shape), dt))

    def ps(name):
        return es.enter_context(nc.psum_tensor(name, [128, 512], F32))

    x_sb = sb("x_sb", [128, KC, T], F32)
    h_sb = sb("h_sb", [128, KC, T], BF16)
    y_sb = sb("y_sb", [128, 4, T], BF16)
    ubuf = sb("ubuf", [128, UW], BF16)
    vbuf = sb("vbuf", [128, VW], BF16)
    wsl = sb("wsl", [128, NW, 1024], BF16)
    mk = sb("mk", [128, 2, WA, 128], BF16)
    sq = sb("sq", [128, 2, 2, 512], BF16)
    d2 = sb("d2", [128, 2, 512], BF16)
    NTH, NSZ, NGZ = 2, 4, 2
    th_t = sb("th_t", [128, NTH, 512], F32)
    sz_t = sb("sz_t", [128, NSZ, 512], F32)
    gz_t = th_t
    lnv_t = sb("lnv_t", [128, 1, 512], F32)
    rstd_t = sb("rstd_t", [128, 1, 512], F32)
    tt_t = sb("tt_t", [128, 1, 512], F32)
    ss_t = sb("ss_t", [128, 1, 512], F32)
    gcs_t = tt_t
    acc = sb("acc", [128, 2, 512], F32)
    accb = sb("accb", [128, 4, 512], BF16)
    cwh = sb("cwh", [128, DEPTH, 8, WA], F32)
    cb16 = sb("cb16", [128, 128], BF16)
    oa_t = sb("oa_t", [128, 8, (WA - 1) + NS * DS], F32)
    ob_t = sb("ob_t", [128, 8, (WB - 1) + NS * DS], F32)
    hista = sb("hista", [128, (WA - 1) * NS], F32)
    histb = sb("histb", [128, DEPTH, 8, (WB - 1) * NS], F32)
    cmat = sb("cmat_sb", [128, 2, 128], F32)
    ones_s = sb("ones_s", [128, 128], BF16)
    ones_v = sb("ones_v", [128, 128], BF16)
    vecs = sb("vecs_sb", [128, 72], F32)
    cbias = sb("cbias", [128, 16], F32)
    cw_a = sb("cw_a_sb", [128, DEPTH, 8, WA], F32)
    cw_b = sb("cw_b_sb", [128, DEPTH, 8, WB], F32)
    banks = [ps(f"psb{i}") for i in range(8)]

    R = Res
    x_r = [[R(f"x{k}_{b}") for b in range(NB)] for k in range(KC)]
    h_r = [[R(f"h{k}_{b}") for b in range(NB)] for k in range(KC)]
    y_r = [[R(f"y{k}_{b}") for b in range(NB)] for k in range(4)]
    u_r = [R(f"u{b}") for b in range(NB)]
    uh_r = R("uhist")
    v_r = [R(f"v{b}") for b in range(NB)]
    vh_r = R("vhist")
    pad_r = R("pads")
    w_r = [R(f"w{i}") for i in range(NW)]
    mk_r = [R("mk0"), R("mk1")]
    dk_r = [R("dk0"), R("dk1")]
    sq_r = [R("sq0"), R("sq1")]
    d2_r = [R("d20"), R("d21")]
    th_r = [R(f"th{i}") for i in range(NTH)]
    sz_r = [R(f"sz{i}") for i in range(NSZ)]
    gz_r = th_r
    lnv_r, rstd_r, tt_r, ss_r = R("lnv"), R("rstd"), R("tt"), R("ss")
    gcs_r = tt_r
    acc_r = [R("acc0"), R("acc1")]
    accb_r = [R(f"accb{i}") for i in range(4)]
    cwh_r = R("cwh")
    cb16_r = R("cb16")
    oa_r = [R(f"oa{j}") for j in range(8)]
    ob_r = [R(f"ob{j}") for j in range(8)]
    hista_r = R("hista")
    histb_r = R("histb")
    const_r = R("consts")
    cbias_r = R("cbias")
    bank_r = [R(f"bank{i}") for i in range(8)]

    class Rot:
        def __init__(self, ids):
            self.ids = list(ids)
            self.i = 0

        def next(self):
            v = self.ids[self.i % len(self.ids)]
            self.i += 1
            return v

    pool_a = Rot([0, 1, 2, 3])
    pool_d = Rot([4, 5, 6, 7])
    rot_th, rot_sz, rot_gz, rot_sq, rot_d2, rot_acc = Rot(range(NTH)), Rot(range(NSZ)), Rot(range(NGZ)), Rot(range(2)), Rot(range(2)), Rot(range(2))

    chans = []

    def chan(name):
        chans.append(name)
        return name

    ch_c = [chan(f"const{i}") for i in range(5)]
    ch_x = [chan(f"x{b}") for b in range(NB)]
    ch_w = [chan(f"w{i}") for i in range(NW)]
    ch_hista = chan("hista")
    ch_out = [chan(f"out{i}") for i in range(NTH + NSZ)]
    ch_oa = chan("oa")
    ch_ob = chan("ob")
    ch_copy = chan("copy")
    ch_dbg = chan("dbg")

    deferred = []
    step_ctr = [0]

    def defer(step_id, rank, fn):
        deferred.append((step_id, rank, fn))

    def after_major():
        if deferred:
            i = min(range(len(deferred)), key=lambda i: deferred[i][:2])
            deferred.pop(i)[2]()

    def pop_rank(rank):
        cand = [i for i in range(len(deferred)) if deferred[i][1] == rank
                and not any(d[0] == deferred[i][0] and d[1] < rank for d in deferred)]
        if cand:
            i = min(cand, key=lambda i: deferred[i][0])
            deferred.pop(i)[2]()

    def pop_step(step_id, rank):
        for i in range(len(deferred)):
            if deferred[i][0] == step_id and deferred[i][1] == rank:
                deferred.pop(i)[2]()
                return

    lag = [None]

    def flush_deferred():
        while deferred:
            after_major()
        lag[0] = None

    P.op("sp", lambda e: e.dma_start(out=vecs[:], in_=vecs_d), writes=[const_r], chan=ch_c[0])
    P.op("sp", lambda e: e.dma_start(out=cmat[:], in_=cmat_d), writes=[const_r], chan=ch_c[1])
    P.op("sp", lambda e: e.dma_start(out=cw_a[:], in_=cw_a_d), writes=[const_r], chan=ch_c[2])
    P.op("sp", lambda e: e.dma_start(out=cw_b[:], in_=cw_b_d), writes=[const_r], chan=ch_c[3])
    P.op("sp", lambda e: e.dma_start(out=histb[:], in_=hist_b_d), writes=[histb_r], chan=ch_c[4])
    for b, (c0, n) in enumerate(BLOCKS):
        P.op("sp", (lambda e, c0=c0, n=n: e.dma_start(out=x_sb[:, :, c0:c0 + n], in_=xT[:, :, c0:c0 + n])),
             writes=[x_r[k][b] for k in range(KC)], chan=ch_x[b])
    for l in range(DEPTH):
        P.op("sp", (lambda e, l=l: e.dma_start(out=oa_copy[l], in_=sa_orig[l, :, DS:WA - 1, :])), chan=ch_copy)
    ones_r = R("ones")
    P.op("dve", lambda e: e.memset(ones_s[:], 1.0 / D), writes=[ones_r])
    P.op("dve", lambda e: e.memset(ones_v[:], 1.0 / 128), writes=[ones_r])
    P.op("dve", lambda e: e.memset(ubuf[:, 0:UH], 0.0), writes=[pad_r])
    P.op("dve", lambda e: e.memset(vbuf[:, 0:VH], 0.0), writes=[pad_r])
    P.op("dve", lambda e: e.tensor_scalar_mul(out=cwh[:], in0=cw_a[:], scalar1=0.5), reads=[const_r], writes=[cwh_r])
    P.op("dve", lambda e: e.tensor_scalar_mul(out=cb16[:], in0=cmat[:, 0, :], scalar1=2.0), reads=[const_r], writes=[cb16_r])
    bk = pool_a.next()
    P.op("pe", (lambda e, bk=bk: e.matmul(banks[bk][:, 0:16], lhsT=cmat[:, 0, :], rhs=vecs[:, 16:32], start=True, stop=True)),
         reads=[const_r], writes=[bank_r[bk]])
    P.op("act", (lambda e, bk=bk: e.activation(out=cbias[:], in_=banks[bk][:, 0:16], func=AF.Copy, scale=2.0)),
         reads=[bank_r[bk]], writes=[cbias_r])

    wseq = []
    for l in range(DEPTH):
        for half in range(2):
            for j in range(4 * half, 4 * half + 4):
                for c in (8 + j, j, 16 + j):
                    wseq.append(w_in_r[l, c])
            for kk in range(4):
                wseq.append(w_out_r[l, 4 * half + kk])
        for half in range(2):
            for j in range(4 * half, 4 * half + 4):
                for c in (48 + j, 24 + j, 32 + j, 40 + j):
                    wseq.append(w_in_r[l, c])
            for kk in range(4):
                wseq.append(w_out_r[l, 8 + 4 * half + kk])
    wstate = {"loaded": 0, "next": 0}

    def w_prefetch(upto):
        while wstate["loaded"] < min(upto, len(wseq)):
            i = wstate["loaded"]
            s = i % NW
            P.op("pool", (lambda e, i=i, s=s: e.dma_start(out=wsl[:, s, :], in_=wseq[i])),
                 writes=[w_r[s]], chan=ch_w[s], name=f"wload{i}")
            wstate["loaded"] += 1

    def w_get_batch(k):
        start = wstate["next"]
        wstate["next"] += k
        w_prefetch(start + NW)
        return [(start + i) % NW for i in range(k)]

    def mm_group(bank, n, pairs, reads, writes_extra=(), out_view=None):
        outap = out_view if out_view is not None else banks[bank][:, 0:n]

        def fn(e):
            ins = None
            for i, (l_, r_) in enumerate(pairs):
                ins = e.matmul(outap, lhsT=l_, rhs=r_, start=(i == 0), stop=(i == len(pairs) - 1))
            return ins
        P.op("pe", fn, reads=reads, writes=[bank_r[bank]] + list(writes_extra))

    def blk_view(ap2d, b, n):
        if b == NB - 1:
            return ap2d.rearrange("p (s i) -> p s i", i=DS)
        return ap2d

    def rms_phase(gcol0, final=False):
        for b in range(NB):
            rms_block(gcol0, b, final)

    def rms_block(gcol0, b, final=False):
        rms_stats(b)
        rms_apply(gcol0, b, final)

    def rms_stats(b, pops=True):
        c0, n = BLOCKS[b]
        bk = pool_d.next()
        for kp in range(4):
            s = rot_sq.next()
            P.op("act", (lambda e, s=s, kp=kp, c0=c0, n=n: e.activation(
                out=sq[:, s, :, 0:n], in_=x_sb[:, 2 * kp:2 * kp + 2, c0:c0 + n], func=AF.Square)),
                reads=[x_r[2 * kp][b], x_r[2 * kp + 1][b]], writes=[sq_r[s]])

            def fn(e, s=s, kp=kp, n=n, bk=bk):
                ins = None
                for i in range(2):
                    ins = e.matmul(banks[bk][:, 0:n], lhsT=ones_s[:], rhs=sq[:, s, i, 0:n],
                                   start=(kp == 0 and i == 0), stop=(kp == 3 and i == 1))
                return ins
            P.op("pe", fn, reads=[sq_r[s], ones_r], writes=[bank_r[bk]])
            if pops:
                after_major()
        P.op("act", (lambda e, bk=bk, n=n: e.activation(out=lnv_t[:, 0, 0:n], in_=banks[bk][:, 0:n], func=AF.Ln, bias=RMS_EPS)),
             reads=[bank_r[bk]], writes=[lnv_r])
        P.op("act", (lambda e, n=n: e.activation(out=rstd_t[:, 0, 0:n], in_=lnv_t[:, 0, 0:n], func=AF.Exp, scale=-0.5)),
             reads=[lnv_r], writes=[rstd_r])

    def rms_apply(gcol0, b, final=False):
        c0, n = BLOCKS[b]
        for k in range(KC):
            if not final:
                P.op("dve", (lambda e, k=k, c0=c0, n=n: e.scalar_tensor_tensor(
                    out=h_sb[:, k, c0:c0 + n], in0=x_sb[:, k, c0:c0 + n], scalar=vecs[:, gcol0 + k:gcol0 + k + 1],
                    in1=rstd_t[:, 0, 0:n], op0=ALU.mult, op1=ALU.mult)),
                    reads=[x_r[k][b], rstd_r, const_r], writes=[h_r[k][b]])
            else:
                oi = (b * KC + k) % (NTH + NSZ)
                if oi < NTH:
                    tile_ap, tres = th_t[:, oi, 0:n], th_r[oi]
                else:
                    tile_ap, tres = sz_t[:, oi - NTH, 0:n], sz_r[oi - NTH]
                P.op("dve", (lambda e, k=k, c0=c0, n=n, tile_ap=tile_ap: e.scalar_tensor_tensor(
                    out=tile_ap, in0=x_sb[:, k, c0:c0 + n], scalar=vecs[:, gcol0 + k:gcol0 + k + 1],
                    in1=rstd_t[:, 0, 0:n], op0=ALU.mult, op1=ALU.mult)),
                    reads=[x_r[k][b], rstd_r, const_r], writes=[tres])
                P.op("sp", (lambda e, k=k, c0=c0, n=n, tile_ap=tile_ap: e.dma_start(out=y_fm[:, k, c0:c0 + n], in_=tile_ap)),
                     reads=[tres], chan=ch_out[oi])

    def out_round(l=0, rnd=0, after_block=None):
        slots = w_get_batch(4)
        st_fn, ap_fn = after_block if after_block is not None else (None, None)
        for b, (c0, n) in enumerate(BLOCKS):
            for m in range(KC):
                bk = pool_a.next()
                pairs = [(wsl[:, slots[kk], m * 128:(m + 1) * 128], y_sb[:, kk, c0:c0 + n]) for kk in range(4)]
                mm_group(bk, n, pairs, reads=[w_r[s] for s in slots] + [y_r[kk][b] for kk in range(4)])
                P.op("dve", (lambda e, bk=bk, m=m, c0=c0, n=n: e.tensor_tensor(
                    out=x_sb[:, m, c0:c0 + n], in0=x_sb[:, m, c0:c0 + n], in1=banks[bk][:, 0:n], op=ALU.add)),
                    reads=[bank_r[bk], x_r[m][b]], writes=[x_r[m][b]])
                if b * KC + m >= DRAIN_DELAY:
                    after_major()
                if ap_fn is not None and m == 3 and b >= 2:
                    ap_fn(b - 2)
            if st_fn is not None and b >= 1:
                st_fn(b - 1)
        if st_fn is not None:
            ap_fn(NB - 2)
            st_fn(NB - 1)
            ap_fn(NB - 1)
        if DEBUG:
            flush_deferred()
            for kk in range(4):
                P.op("sp", (lambda e, kk=kk: e.dma_start(out=dbg_y[:, l, 4 * rnd + kk, :], in_=y_sb[:, kk, SEQ:T])),
                     reads=[y_r[kk][NB - 1]], chan=ch_dbg)
            if rnd == 3:
                P.op("sp", (lambda e: e.dma_start(out=dbg_x[:, l], in_=x_sb[:, :, SEQ:T])),
                     reads=[x_r[m][NB - 1] for m in range(KC)], chan=ch_dbg)

    def a_head_prep(l, j):
        s = j % 2
        P.op("dve", (lambda e, s=s, l=l, j=j: e.tensor_tensor(
            out=mk[:, s, N_DVE:WA, :], in0=cmat[:, 0, :].unsqueeze(1).broadcast_to([128, WA - N_DVE, 128]),
            in1=cw_a[:, l, j, N_DVE:WA].unsqueeze(2).broadcast_to([128, WA - N_DVE, 128]), op=ALU.mult)),
            reads=[const_r], writes=[mk_r[s]])

    def a_hist(l, j):
        P.op("sp", (lambda e, l=l, j=j: e.dma_start(out=hista[:], in_=hist_a_d[:, l, j])), writes=[hista_r], chan=ch_hista)
        P.op("dve", (lambda e: e.tensor_scalar_mul(out=ubuf[:, US:US + (WA - 1) * NS], in0=hista[:], scalar1=2.0)),
             reads=[hista_r], writes=[uh_r])

    def a_step(l, j, b, jj, slots):
        c0, n = BLOCKS[b]
        sg, sv, sz_w = slots
        hreads = [h_r[k][b] for k in range(KC)]
        bg = pool_a.next()
        mm_group(bg, n, [(wsl[:, sg, k * 128:(k + 1) * 128], h_sb[:, k, c0:c0 + n]) for k in range(KC)], reads=[w_r[sg]] + hreads)
        pop_rank(1)
        ith = rot_th.next()
        P.op("act", (lambda e, bg=bg, ith=ith, n=n: e.activation(out=th_t[:, ith, 0:n], in_=banks[bg][:, 0:n], func=AF.Tanh, scale=0.5)),
             reads=[bank_r[bg]], writes=[th_r[ith]], name=f"th{j}.{b}")
        bv = pool_a.next()
        mm_group(bv, n, [(wsl[:, sv, k * 128:(k + 1) * 128], h_sb[:, k, c0:c0 + n]) for k in range(KC)], reads=[w_r[sv]] + hreads)
        if b < NB - 1:
            uout = ubuf[:, UH + c0:UH + c0 + n]
        else:
            uout = ubuf[:, US + (WA - 1) * NS:UW]
        P.op("dve", (lambda e, bv=bv, ith=ith, n=n, uout=uout, b=b: e.scalar_tensor_tensor(
            out=uout, in0=th_t[:, ith, 0:n], scalar=1.0, in1=banks[bv][:, 0:n],
            op0=ALU.add, op1=ALU.mult)),
            reads=[bank_r[bv], th_r[ith]], writes=[u_r[b]])
        if b == NB - 2:
            P.op("dve", (lambda e, bv=bv, ith=ith, n=n, j=j: e.scalar_tensor_tensor(
                out=oa_t[:, j, 0:WA - 1], in0=th_t[:, ith, n - (WA - 1):n], scalar=1.0, in1=banks[bv][:, n - (WA - 1):n],
                op0=ALU.add, op1=ALU.mult)),
                reads=[bank_r[bv], th_r[ith]], writes=[oa_r[j]])
        if b == NB - 1:
            P.op("dve", (lambda e, bv=bv, ith=ith, n=n, j=j: e.scalar_tensor_tensor(
                out=oa_t[:, j, WA - 1:WA - 1 + n], in0=th_t[:, ith, 0:n], scalar=1.0, in1=banks[bv][:, 0:n],
                op0=ALU.add, op1=ALU.mult)),
                reads=[bank_r[bv], th_r[ith]], writes=[oa_r[j]])
        pop_rank(2)
        bz = pool_a.next()
        mm_group(bz, n, [(wsl[:, sz_w, k * 128:(k + 1) * 128], h_sb[:, k, c0:c0 + n]) for k in range(KC)], reads=[w_r[sz_w]] + hreads)
        pop_rank(3)
        isz = rot_sz.next()
        P.op("act", (lambda e, bz=bz, isz=isz, n=n: e.activation(out=sz_t[:, isz, 0:n], in_=banks[bz][:, 0:n], func=AF.Silu)),
             reads=[bank_r[bz]], writes=[sz_r[isz]], name=f"sz{j}.{b}")
        if b < NB - 1:
            pop_rank(0)
            if b == 0:
                pop_rank(0)
        bd = pool_d.next()
        ms = j % 2
        if b < NB - 1:
            def usl(k):
                return ubuf[:, c0 + k:c0 + k + n]
            creads = [u_r[b]] + ([u_r[b - 1]] if b > 0 else [pad_r])
        else:
            def usl(k):
                return ubuf[:, US + NS * k:US + NS * k + n]
            creads = [u_r[b], uh_r]
        nd = N_DVE_PRE_SAMPLE if b == NB - 2 else N_DVE
        pe_taps = list(range(nd, WA))

        def pe_fn(e):
            ins = None
            for i, k in enumerate(pe_taps):
                ins = e.matmul(banks[bd][:, 0:n], lhsT=mk[:, ms, k, :], rhs=usl(k), start=(i == 0), stop=(nd == 0 and i == len(pe_taps) - 1))
            return ins
        P.op("pe", pe_fn, reads=[mk_r[ms]] + creads, writes=[bank_r[bd]])
        iac = rot_acc.next()

        def dve_taps(k0, k1):
            chains = [[k for k in range(k0, k1) if k % 2 == 0], [k for k in range(k0, k1) if k % 2 == 1]]
            order = []
            for i in range(max(len(chains[0]), len(chains[1]))):
                for c in range(2):
                    if i < len(chains[c]):
                        order.append((c, i, chains[c][i]))
            for c, i, k in order:
                wcol = cwh[:, l, j, k:k + 1]
                last = (i == len(chains[c]) - 1)
                outap = accb[:, 2 * iac + c, 0:n] if last else acc[:, c, 0:n]
                wres = accb_r[2 * iac + c] if last else acc_r[c]
                if i == 0:
                    P.op("dve", (lambda e, k=k, wcol=wcol, outap=outap: e.tensor_scalar_mul(out=outap, in0=usl(k), scalar1=wcol)),
                         reads=creads + [cwh_r], writes=[wres])
                else:
                    P.op("dve", (lambda e, k=k, wcol=wcol, outap=outap, c=c: e.scalar_tensor_tensor(
                        out=outap, in0=usl(k), scalar=wcol, in1=acc[:, c, 0:n], op0=ALU.mult, op1=ALU.add)),
                        reads=creads + [cwh_r, acc_r[c]], writes=[wres])
            return [c for c in range(2) if chains[c]]
        used_chains = dve_taps(0, nd)
        pop_rank(4)
        id2 = rot_d2.next()
        cbc = cbias[:, l * 8 + j:l * 8 + j + 1]
        bvar_box = {}

        def stage0():
            for ci, c in enumerate(used_chains):
                P.op("pe", (lambda e, c=c, ci=ci: e.matmul(banks[bd][:, 0:n], lhsT=cb16[:], rhs=accb[:, 2 * iac + c, 0:n],
                                                        start=False, stop=(ci == len(used_chains) - 1))),
                     reads=[accb_r[2 * iac + c], cb16_r], writes=[bank_r[bd]])
            P.op("act", (lambda e: e.activation(out=d2[:, id2, 0:n], in_=banks[bd][:, 0:n], func=AF.Square, bias=cbc)),
                 reads=[bank_r[bd], cbias_r], writes=[d2_r[id2]], name=f"d2_{j}.{b}")

        def stage1():
            bvv = pool_a.next()
            bvar_box["b"] = bvv
            mm_group(bvv, n, [(ones_v[:], d2[:, id2, 0:n])], reads=[d2_r[id2], ones_r])

        def stage2():
            bvv = bvar_box["b"]
            P.op("act", (lambda e: e.activation(out=lnv_t[:, 0, 0:n], in_=banks[bvv][:, 0:n], func=AF.Ln, bias=LN_EPS)),
                 reads=[bank_r[bvv]], writes=[lnv_r], name=f"ln{j}.{b}")
            P.op("act", (lambda e: e.activation(out=rstd_t[:, 0, 0:n], in_=lnv_t[:, 0, 0:n], func=AF.Exp, scale=-0.5)),
                 reads=[lnv_r], writes=[rstd_r], name=f"exp{j}.{b}")
            P.op("dve", (lambda e: e.scalar_tensor_tensor(out=tt_t[:, 0, 0:n], in0=banks[bd][:, 0:n], scalar=cbc, in1=rstd_t[:, 0, 0:n],
                                                            op0=ALU.add, op1=ALU.mult)),
                 reads=[bank_r[bd], rstd_r, cbias_r], writes=[tt_r])

        def stage3():
            gcol = vecs[:, 32 + l * 8 + j:32 + l * 8 + j + 1]
            bcol = vecs[:, 48 + l * 8 + j:48 + l * 8 + j + 1]
            P.op("act", (lambda e: e.activation(out=ss_t[:, 0, 0:n], in_=tt_t[:, 0, 0:n], func=AF.Silu, scale=gcol, bias=bcol)),
                 reads=[tt_r, const_r], writes=[ss_r], name=f"s{j}.{b}")

        def stage4():
            P.op("dve", (lambda e: e.tensor_tensor(out=y_sb[:, jj, c0:c0 + n], in0=ss_t[:, 0, 0:n], in1=sz_t[:, isz, 0:n], op=ALU.mult)),
                 reads=[ss_r, sz_r[isz]], writes=[y_r[jj][b]])
        if b == 1 and lag[0] is not None:
            for rk in (1, 2, 3, 4):
                pop_step(lag[0], rk)
            lag[0] = None
        sid = step_ctr[0]
        step_ctr[0] += 1
        for rk, fn in enumerate([stage0, stage1, stage2, stage3, stage4]):
            defer(sid, rk, fn)
        if b == NB - 1:
            lag[0] = sid

    def a_finish(l):
        P.op("dve", (lambda e: e.tensor_scalar_mul(out=oa_t[:], in0=oa_t[:], scalar1=0.5)),
             reads=oa_r, writes=oa_r)
        P.op("sp", (lambda e, l=l: e.dma_start(out=oa_p[:, l], in_=oa_t[:, :, 0:WA - 1])), reads=oa_r, chan=ch_oa)
        P.op("sp", (lambda e, l=l: e.dma_start(out=oa_new[:, l], in_=oa_t[:, :, WA - 1:WA - 1 + NS * DS])), reads=oa_r, chan=ch_oa)

    def b_hist(l, j):
        P.op("dve", (lambda e, l=l, j=j: e.tensor_copy(vbuf[:, VS:VS + (WB - 1) * NS], histb[:, l, j, :])),
             reads=[histb_r], writes=[vh_r])

    def b_step(l, j, b, jj, slots):
        c0, n = BLOCKS[b]
        s_zb, s_gb, s_gc, s_hb = slots
        hreads = [h_r[k][b] for k in range(KC)]

        def inproj(slot):
            bk = pool_a.next()
            mm_group(bk, n, [(wsl[:, slot, k * 128:(k + 1) * 128], h_sb[:, k, c0:c0 + n]) for k in range(KC)], reads=[w_r[slot]] + hreads)
            after_major()
            return bk
        bzb = inproj(s_zb)
        if b == 0:
            b_hist(l, j)
        isz = rot_sz.next()
        P.op("act", (lambda e: e.activation(out=sz_t[:, isz, 0:n], in_=banks[bzb][:, 0:n], func=AF.Silu)),
             reads=[bank_r[bzb]], writes=[sz_r[isz]])
        bgb = inproj(s_gb)
        igz = rot_gz.next()
        P.op("dve", (lambda e: e.tensor_tensor(out=gz_t[:, igz, 0:n], in0=banks[bgb][:, 0:n], in1=sz_t[:, isz, 0:n], op=ALU.mult)),
             reads=[bank_r[bgb], sz_r[isz]], writes=[gz_r[igz]])
        bgc = inproj(s_gc)
        P.op("act", (lambda e: e.activation(out=gcs_t[:, 0, 0:n], in_=banks[bgc][:, 0:n], func=AF.Copy)),
             reads=[bank_r[bgc]], writes=[gcs_r])
        bhb = inproj(s_hb)
        if b < NB - 1:
            vout = vbuf[:, VH + c0:VH + c0 + n]
        else:
            vout = vbuf[:, VS + (WB - 1) * NS:VW]
        P.op("dve", (lambda e: e.tensor_tensor(out=vout, in0=banks[bhb][:, 0:n], in1=gcs_t[:, 0, 0:n], op=ALU.mult)),
             reads=[bank_r[bhb], gcs_r], writes=[v_r[b]])
        if b == NB - 2:
            P.op("dve", (lambda e: e.tensor_tensor(out=ob_t[:, j, 0:WB - 1], in0=banks[bhb][:, n - (WB - 1):n], in1=gcs_t[:, 0, n - (WB - 1):n], op=ALU.mult)),
                 reads=[bank_r[bhb], gcs_r], writes=[ob_r[j]])
        if b == NB - 1:
            P.op("dve", (lambda e: e.tensor_tensor(out=ob_t[:, j, WB - 1:WB - 1 + n], in0=banks[bhb][:, 0:n], in1=gcs_t[:, 0, 0:n], op=ALU.mult)),
                 reads=[bank_r[bhb], gcs_r], writes=[ob_r[j]])

        if b < NB - 1:
            def vsl(k):
                return vbuf[:, c0 + k:c0 + k + n]
            creads = [v_r[b]] + ([v_r[b - 1]] if b > 0 else [pad_r])
        else:
            def vsl(k):
                return vbuf[:, VS + NS * k:VS + NS * k + n]
            creads = [v_r[b], vh_r]
        iac = rot_acc.next()
        for k in range(WB):
            wcol = cw_b[:, l, j, k:k + 1]
            if k == 0:
                P.op("dve", (lambda e, k=k, wcol=wcol: e.tensor_scalar_mul(out=acc[:, iac, 0:n], in0=vsl(k), scalar1=wcol)),
                     reads=creads + [const_r], writes=[acc_r[iac]], relaxed=(acc_r[iac],))
            else:
                P.op("dve", (lambda e, k=k, wcol=wcol: e.scalar_tensor_tensor(
                    out=acc[:, iac, 0:n], in0=vsl(k), scalar=wcol, in1=acc[:, iac, 0:n], op0=ALU.mult, op1=ALU.add)),
                    reads=creads + [const_r, acc_r[iac]], writes=[acc_r[iac]], relaxed=(acc_r[iac],))
        P.op("dve", (lambda e: e.tensor_tensor(out=y_sb[:, jj, c0:c0 + n], in0=acc[:, iac, 0:n], in1=gz_t[:, igz, 0:n], op=ALU.mult)),
             reads=[acc_r[iac], gz_r[igz]], writes=[y_r[jj][b]], relaxed=(acc_r[iac],))

    def b_finish(l):
        P.op("sp", (lambda e, l=l: e.dma_start(out=ob_p[:, l], in_=ob_t[:, :, 0:WB - 1])), reads=ob_r, chan=ch_ob)
        P.op("sp", (lambda e, l=l: e.dma_start(out=ob_new[:, l], in_=ob_t[:, :, WB - 1:WB - 1 + NS * DS])), reads=ob_r, chan=ch_ob)

    w_prefetch(NW)
    for l in range(DEPTH):
        if l == 0:
            rms_phase(0)
            a_head_prep(l, 0)
        for half in range(2):
            for j in range(4 * half, 4 * half + 4):
                slots = w_get_batch(3)
                if j + 1 < 8:
                    a_head_prep(l, j + 1)
                a_hist(l, j)
                for b in range(NB):
                    a_step(l, j, b, j - 4 * half, slots)
            out_round(l, half)
            flush_deferred()
        a_finish(l)
        for half in range(2):
            for j in range(4 * half, 4 * half + 4):
                slots = w_get_batch(4)
                for b in range(NB):
                    b_step(l, j, b, j - 4 * half, slots)
            nxt = None
            if half == 1 and l + 1 < DEPTH:
                a_head_prep(l + 1, 0)
            if half == 1:
                if l + 1 < DEPTH:
                    nxt = ((lambda b: rms_stats(b, pops=False)), (lambda b, l=l: rms_apply((l + 1) * 8, b)))
                else:
                    nxt = ((lambda b: rms_stats(b, pops=False)), (lambda b: rms_apply(64, b, final=True)))
            out_round(l, 2 + half, after_block=nxt)
            flush_deferred()
        b_finish(l)

    final_waits = ch_out + [ch_oa, ch_ob, ch_copy] + ([ch_dbg] if DEBUG else [])
    sem_ctx = {}
    for e in Prog.ENGS:
        sem_ctx[e] = es.enter_context(nc.semaphore(f"s_{e}"))
    chan_sems = {c: es.enter_context(nc.semaphore(f"c_{c}")) for c in chans}
    block = es.enter_context(nc.Block())
    P.emit(nc, block, sem_ctx, chan_sems, final_waits)
    es.close()
    return nc


_NC_CACHE = {}


def _get_nc():
    if "nc" not in _NC_CACHE:
        _NC_CACHE["nc"] = build_nc()
    return _NC_CACHE["nc"]


def _prep_shared(norm_g, w_in, conv_a_w, conv_a_b, ln_a_g, ln_a_b, conv_b_w, w_out, final_g):
    f = np.float32
    w_in_r = np.ascontiguousarray(
        np.asarray(w_in, f).reshape(DEPTH, KC, 128, 56, 128).transpose(0, 3, 2, 1, 4)).reshape(DEPTH, 56, 128, 1024)
    w_out_r = np.ascontiguousarray(np.asarray(w_out, f).reshape(DEPTH, 16, 128, 1024))
    cw_a = np.ascontiguousarray(np.asarray(conv_a_w, f).reshape(DEPTH, WA, 8, 128).transpose(3, 0, 2, 1))
    cw_b = np.ascontiguousarray(np.asarray(conv_b_w, f).reshape(DEPTH, WB, 8, 128).transpose(3, 0, 2, 1))

    def pv(v):
        return np.asarray(v, f).reshape(-1, 8, 128).transpose(2, 0, 1).reshape(128, -1)
    vecs = np.ascontiguousarray(np.concatenate(
        [pv(norm_g), pv(conv_a_b), pv(ln_a_g), pv(ln_a_b), pv(np.asarray(final_g, f)[None])], axis=1))
    cmat = np.zeros((128, 2, 128), f)
    cmat[:, 0, :] = 0.5 * (np.eye(128, dtype=f) - f(1.0 / 128))
    cmat[:, 1, :] = np.eye(128, dtype=f)
    return dict(w_in_r=w_in_r, w_out_r=w_out_r, cw_a=cw_a, cw_b=cw_b, vecs=vecs, cmat=cmat)


def kernel(x_prompt, x_sample, state_conv_a, state_conv_b, norm_g, w_in, conv_a_w, conv_a_b,
           ln_a_g, ln_a_b, conv_b_w, w_out, final_g):
    f = np.float32
    nc = _get_nc()
    shared = _prep_shared(norm_g, w_in, conv_a_w, conv_a_b, ln_a_g, ln_a_b, conv_b_w, w_out, final_g)
    x_prompt = np.asarray(x_prompt, f)
    x_sample = np.asarray(x_sample, f)
    state_conv_a = np.asarray(state_conv_a, f)
    state_conv_b = np.asarray(state_conv_b, f)
    in_maps = []
    for c in range(N_CORES):
        xs = x_sample[NS * c:NS * (c + 1)].transpose(1, 0, 2).reshape(NS * DS, D)
        xt = np.concatenate([x_prompt[c], xs], axis=0)
        xT = np.ascontiguousarray(xt.reshape(T, KC, 128).transpose(2, 1, 0))
        sa = state_conv_a[:, NS * c:NS * (c + 1)]
        sbb = state_conv_b[:, NS * c:NS * (c + 1)]
        hist_a = np.ascontiguousarray(sa.reshape(DEPTH, NS, WA - 1, 8, 128).transpose(4, 0, 3, 2, 1)).reshape(128, DEPTH, 8, (WA - 1) * NS)
        hist_b = np.ascontiguousarray(sbb.reshape(DEPTH, NS, WB - 1, 8, 128).transpose(4, 0, 3, 2, 1)).reshape(128, DEPTH, 8, (WB - 1) * NS)
        m = dict(shared)
        m.update(xT=xT, hist_a=hist_a, hist_b=hist_b, sa_orig=np.ascontiguousarray(sa))
        in_maps.append(m)
    res = run_bass_kernel_spmd(nc, in_maps, core_ids=list(range(N_CORES)))
    y_prompt = np.empty((N_CORES, SEQ, D), f)
    y_sample = np.empty((N_CORES * NS, DS, D), f)
    na_p = np.empty((DEPTH, N_CORES, WA - 1, D), f)
    nb_p = np.empty((DEPTH, N_CORES, WB - 1, D), f)
    na_s = np.empty((DEPTH, N_CORES * NS, WA - 1, D), f)
    nb_s = np.empty((DEPTH, N_CORES * NS, WB - 1, D), f)
    for c in range(N_CORES):
        r = res.results[c]
        yfm = np.asarray(r["y_fm"], f)
        ytm = yfm.transpose(2, 1, 0).reshape(T, D)
        y_prompt[c] = ytm[:SEQ]
        y_sample[NS * c:NS * (c + 1)] = ytm[SEQ:].reshape(DS, NS, D).transpose(1, 0, 2)
        oap = np.asarray(r["oa_p"], f)
        na_p[:, c] = oap.transpose(1, 3, 2, 0).reshape(DEPTH, WA - 1, D)
        obp = np.asarray(r["ob_p"], f)
        nb_p[:, c] = obp.transpose(1, 3, 2, 0).reshape(DEPTH, WB - 1, D)
        na_s[:, NS * c:NS * (c + 1), :WA - 1 - DS] = np.asarray(r["oa_copy"], f)
        oan = np.asarray(r["oa_new"], f).reshape(128, DEPTH, 8, DS, NS)
        na_s[:, NS * c:NS * (c + 1), WA - 1 - DS:] = oan.transpose(1, 4, 3, 2, 0).reshape(DEPTH, NS, DS, D)
        obn = np.asarray(r["ob_new"], f).reshape(128, DEPTH, 8, DS, NS)
        nb_s[:, NS * c:NS * (c + 1)] = obn.transpose(1, 4, 3, 2, 0).reshape(DEPTH, NS, DS, D)[:, :, DS - (WB - 1):]
    if DEBUG:
        _NC_CACHE["dbg"] = [(np.asarray(r["dbg_y"]).astype(f), np.asarray(r["dbg_x"], f)) for r in res.results]
    return (y_prompt, y_sample, na_p, nb_p, na_s, nb_s)
```

```python
from contextlib import ExitStack
import numpy as np
import concourse.bass as bass
import concourse.mybir as mybir
from concourse.bass_utils import run_bass_kernel_spmd

F32 = mybir.dt.float32
BF16 = mybir.dt.bfloat16
AF = mybir.ActivationFunctionType
ALU = mybir.AluOpType

N_CORES = 8
D = 1024
DEPTH = 2
SEQ = 2048
NS = 16
DS = 4
T = SEQ + NS * DS
KC = 8
WA = 31
WB = 3
RMS_EPS = 1e-6
LN_EPS = 1e-5
BLOCKS = [(0, 512), (512, 512), (1024, 512), (1536, 512), (2048, 64)]
NB = len(BLOCKS)
NW = 9
UH = 30
US = UH + SEQ
UW = US + NS * (WA - 1 + DS)
VH = 2
VS = VH + SEQ
VW = VS + NS * (WB - 1 + DS)
K_SAME = 3
DRAIN_DELAY = 6
RELAX_INPLACE = False
N_DVE = 12
N_DVE_PRE_SAMPLE = 8
MK0 = min(N_DVE, N_DVE_PRE_SAMPLE)


class Res:
    __slots__ = ("name", "w", "r")

    def __init__(self, name):
        self.name = name
        self.w = None
        self.r = []


class Op:
    __slots__ = ("eng", "fn", "deps", "idx", "sig", "cnt", "chan", "name")


class Prog:
    ENGS = ("pe", "act", "dve", "pool", "sp")

    def __init__(self):
        self.ops = {e: [] for e in self.ENGS}
        self.chan_ops = {}

    def op(self, eng, fn, reads=(), writes=(), chan=None, name="", relaxed=()):
        if not RELAX_INPLACE:
            relaxed = ()
        o = Op()
        o.eng, o.fn, o.chan, o.name = eng, fn, chan, name
        o.sig = False
        o.cnt = 0
        o.idx = len(self.ops[eng])
        deps = set()
        for r in reads:
            if r.w is not None and not (r in relaxed and r.w.eng == eng and r.w.chan is None):
                deps.add(r.w)
        for w in writes:
            rel = w in relaxed
            if w.w is not None and not (rel and w.w.eng == eng and w.w.chan is None):
                deps.add(w.w)
            for x in w.r:
                if not (rel and x.eng == eng and x.chan is None):
                    deps.add(x)
        deps.discard(o)
        keep = []
        for d in deps:
            if d.chan is None and chan is None and d.eng == eng:
                if eng == "pe":
                    continue
                if o.idx - d.idx > K_SAME:
                    continue
            keep.append(d)
            d.sig = True
        o.deps = keep
        for r in reads:
            r.r.append(o)
        for w in writes:
            w.w = o
            w.r = []
        self.ops[eng].append(o)
        if chan is not None:
            self.chan_ops.setdefault(chan, []).append(o)
        return o

    def emit(self, nc, block, sems, chan_sems, final_waits):
        for e in self.ENGS:
            c = 0
            for o in self.ops[e]:
                if o.chan is None and o.sig:
                    c += 1
                    o.cnt = c
        for ch, lst in self.chan_ops.items():
            c = 0
            for o in lst:
                c += 16
                o.cnt = c
        chan_total = {ch: 16 * len(lst) for ch, lst in self.chan_ops.items()}

        def run(eng_name):
            def body(e):
                waited = {}
                for o in self.ops[eng_name]:
                    need = {}
                    for d in o.deps:
                        s = chan_sems[d.chan] if d.chan is not None else sems[d.eng]
                        if need.get(s.num, (None, 0))[1] < d.cnt:
                            need[s.num] = (s, d.cnt)
                    for num, (s, c) in need.items():
                        if waited.get(num, 0) < c:
                            e.wait_ge(s, c)
                            waited[num] = c
                    ins = o.fn(e)
                    if o.chan is not None:
                        ins.then_inc(chan_sems[o.chan], 16)
                    elif o.sig:
                        ins.then_inc(sems[eng_name], 1)
                if eng_name == "sp":
                    for ch in final_waits:
                        e.wait_ge(chan_sems[ch], chan_total[ch])
            return body

        block.tensor(run("pe"))
        block.scalar(run("act"))
        block.vector(run("dve"))
        block.gpsimd(run("pool"))
        block.sync(run("sp"))


DEBUG = False


def build_nc():
    nc = bass.Bass("TRN2", target_bir_lowering=False)
    P = Prog()
    es = ExitStack()

    def din(name, shape, dt=F32):
        return nc.dram_tensor(name, list(shape), dt, kind="ExternalInput").ap()

    def dout(name, shape, dt=F32):
        return nc.dram_tensor(name, list(shape), dt, kind="ExternalOutput").ap()

    xT = din("xT", [128, KC, T])
    w_in_r = din("w_in_r", [DEPTH, 56, 128, 1024])
    w_out_r = din("w_out_r", [DEPTH, 16, 128, 1024])
    cw_a_d = din("cw_a", [128, DEPTH, 8, WA])
    cw_b_d = din("cw_b", [128, DEPTH, 8, WB])
    vecs_d = din("vecs", [128, 72])
    hist_a_d = din("hist_a", [128, DEPTH, 8, (WA - 1) * NS])
    hist_b_d = din("hist_b", [128, DEPTH, 8, (WB - 1) * NS])
    sa_orig = din("sa_orig", [DEPTH, NS, WA - 1, D])
    cmat_d = din("cmat", [128, 2, 128])

    y_fm = dout("y_fm", [128, KC, T])
    oa_p = dout("oa_p", [128, DEPTH, 8, WA - 1])
    ob_p = dout("ob_p", [128, DEPTH, 8, WB - 1])
    oa_copy = dout("oa_copy", [DEPTH, NS, WA - 1 - DS, D])
    oa_new = dout("oa_new", [128, DEPTH, 8, NS * DS])
    ob_new = dout("ob_new", [128, DEPTH, 8, NS * DS])

    if DEBUG:
        dbg_y = dout("dbg_y", [128, DEPTH, 16, NS * DS], BF16)
        dbg_x = dout("dbg_x", [128, DEPTH, KC, NS * DS], F32)

    def sb(name, shape, dt):
        return es.enter_context(nc.sbuf_tensor(name, list(shape), dt))

    def ps(name):
        return es.enter_context(nc.psum_tensor(name, [128, 512], F32))

    x_sb = sb("x_sb", [128, KC, T], F32)
    h_sb = sb("h_sb", [128, KC, T], BF16)
    y_sb = sb("y_sb", [128, 4, T], BF16)
    ubuf = sb("ubuf", [128, UW], BF16)
    vbuf = sb("vbuf", [128, VW], BF16)
    wsl = sb("wsl", [128, NW, 1024], BF16)
    mk = sb("mk", [128, 2, WA, 128], BF16)
    dk = sb("dk", [128, 2, WB, 128], BF16)
    sq = sb("sq", [128, 2, 2, 512], BF16)
    d2 = sb("d2", [128, 2, 512], BF16)
    NTH, NSZ, NGZ = 2, 3, 2
    th_t = sb("th_t", [128, NTH, 512], F32)
    sz_t = sb("sz_t", [128, NSZ, 512], F32)
    gz_t = th_t
    lnv_t = sb("lnv_t", [128, 1, 512], F32)
    rstd_t = sb("rstd_t", [128, 1, 512], F32)
    tt_t = sb("tt_t", [128, 1, 512], F32)
    ss_t = sb("ss_t", [128, 1, 512], F32)
    gcs_t = tt_t
    acc = sb("acc", [128, 2, 512], F32)
    accb = sb("accb", [128, 4, 512], BF16)
    cwh = sb("cwh", [128, DEPTH, 8, WA], F32)
    cb16 = sb("cb16", [128, 128], BF16)
    oa_t = sb("oa_t", [128, 8, (WA - 1) + NS * DS], F32)
    ob_t = sb("ob_t", [128, 8, (WB - 1) + NS * DS], F32)
    hista = sb("hista", [128, (WA - 1) * NS], F32)
    histb = sb("histb", [128, DEPTH, 8, (WB - 1) * NS], F32)
    cmat = sb("cmat_sb", [128, 2, 128], F32)
    ones_s = sb("ones_s", [128, 128], BF16)
    ones_v = sb("ones_v", [128, 128], BF16)
    vecs = sb("vecs_sb", [128, 72], F32)
    cbias = sb("cbias", [128, 16], F32)
    cw_a = sb("cw_a_sb", [128, DEPTH, 8, WA], F32)
    cw_b = sb("cw_b_sb", [128, DEPTH, 8, WB], F32)
    banks = [ps(f"psb{i}") for i in range(8)]

    R = Res
    x_r = [[R(f"x{k}_{b}") for b in range(NB)] for k in range(KC)]
    h_r = [[R(f"h{k}_{b}") for b in range(NB)] for k in range(KC)]
    y_r = [[R(f"y{k}_{b}") for b in range(NB)] for k in range(4)]
    u_r = [R(f"u{b}") for b in range(NB)]
    uh_r = R("uhist")
    v_r = [R(f"v{b}") for b in range(NB)]
    vh_r = R("vhist")
    pad_r = R("pads")
    w_r = [R(f"w{i}") for i in range(NW)]
    mk_r = [R("mk0"), R("mk1")]
    dk_r = [R("dk0"), R("dk1")]
    sq_r = [R("sq0"), R("sq1")]
    d2_r = [R("d20"), R("d21")]
    th_r = [R(f"th{i}") for i in range(NTH)]
    sz_r = [R(f"sz{i}") for i in range(NSZ)]
    gz_r = th_r
    lnv_r, rstd_r, tt_r, ss_r = R("lnv"), R("rstd"), R("tt"), R("ss")
    gcs_r = tt_r
    acc_r = [R("acc0"), R("acc1")]
    accb_r = [R(f"accb{i}") for i in range(4)]
    cwh_r = R("cwh")
    cb16_r = R("cb16")
    oa_r = [R(f"oa{j}") for j in range(8)]
    ob_r = [R(f"ob{j}") for j in range(8)]
    hista_r = R("hista")
    histb_r = R("histb")
    const_r = R("consts")
    cbias_r = R("cbias")
    bank_r = [R(f"bank{i}") for i in range(8)]

    class Rot:
        def __init__(self, ids):
            self.ids = list(ids)
            self.i = 0

        def next(self):
            v = self.ids[self.i % len(self.ids)]
            self.i += 1
            return v

    pool_a = Rot([0, 1, 2, 3])
    pool_d = Rot([4, 5, 6, 7])
    rot_th, rot_sz, rot_gz, rot_sq, rot_d2, rot_acc = Rot(range(NTH)), Rot(range(NSZ)), Rot(range(NGZ)), Rot(range(2)), Rot(range(2)), Rot(range(2))

    chans = []

    def chan(name):
        chans.append(name)
        return name

    ch_c = [chan(f"const{i}") for i in range(5)]
    ch_x = [chan(f"x{b}") for b in range(NB)]
    ch_w = [chan(f"w{i}") for i in range(NW)]
    ch_hista = chan("hista")
    ch_out = [chan(f"out{i}") for i in range(NTH + NSZ)]
    ch_oa = chan("oa")
    ch_ob = chan("ob")
    ch_copy = chan("copy")
    ch_dbg = chan("dbg")

    deferred = []
    step_ctr = [0]

    def defer(step_id, rank, fn):
        deferred.append((step_id, rank, fn))

    def after_major():
        if deferred:
            i = min(range(len(deferred)), key=lambda i: deferred[i][:2])
            deferred.pop(i)[2]()

    def pop_rank(rank):
        cand = [i for i in range(len(deferred)) if deferred[i][1] == rank
                and not any(d[0] == deferred[i][0] and d[1] < rank for d in deferred)]
        if cand:
            i = min(cand, key=lambda i: deferred[i][0])
            deferred.pop(i)[2]()

    def flush_deferred():
        while deferred:
            after_major()

    P.op("sp", lambda e: e.dma_start(out=vecs[:], in_=vecs_d), writes=[const_r], chan=ch_c[0])
    P.op("sp", lambda e: e.dma_start(out=cmat[:], in_=cmat_d), writes=[const_r], chan=ch_c[1])
    P.op("sp", lambda e: e.dma_start(out=cw_a[:], in_=cw_a_d), writes=[const_r], chan=ch_c[2])
    P.op("sp", lambda e: e.dma_start(out=cw_b[:], in_=cw_b_d), writes=[const_r], chan=ch_c[3])
    P.op("sp", lambda e: e.dma_start(out=histb[:], in_=hist_b_d), writes=[histb_r], chan=ch_c[4])
    for b, (c0, n) in enumerate(BLOCKS):
        P.op("sp", (lambda e, c0=c0, n=n: e.dma_start(out=x_sb[:, :, c0:c0 + n], in_=xT[:, :, c0:c0 + n])),
             writes=[x_r[k][b] for k in range(KC)], chan=ch_x[b])
    for l in range(DEPTH):
        P.op("sp", (lambda e, l=l: e.dma_start(out=oa_copy[l], in_=sa_orig[l, :, DS:WA - 1, :])), chan=ch_copy)
    ones_r = R("ones")
    P.op("dve", lambda e: e.memset(ones_s[:], 1.0 / D), writes=[ones_r])
    P.op("dve", lambda e: e.memset(ones_v[:], 1.0 / 128), writes=[ones_r])
    P.op("dve", lambda e: e.memset(ubuf[:, 0:UH], 0.0), writes=[pad_r])
    P.op("dve", lambda e: e.memset(vbuf[:, 0:VH], 0.0), writes=[pad_r])
    P.op("dve", lambda e: e.tensor_scalar_mul(out=cwh[:], in0=cw_a[:], scalar1=0.5), reads=[const_r], writes=[cwh_r])
    P.op("dve", lambda e: e.tensor_scalar_mul(out=cb16[:], in0=cmat[:, 0, :], scalar1=2.0), reads=[const_r], writes=[cb16_r])
    bk = pool_a.next()
    P.op("pe", (lambda e, bk=bk: e.matmul(banks[bk][:, 0:16], lhsT=cmat[:, 0, :], rhs=vecs[:, 16:32], start=True, stop=True)),
         reads=[const_r], writes=[bank_r[bk]])
    P.op("act", (lambda e, bk=bk: e.activation(out=cbias[:], in_=banks[bk][:, 0:16], func=AF.Copy, scale=2.0)),
         reads=[bank_r[bk]], writes=[cbias_r])

    wseq = []
    for l in range(DEPTH):
        for half in range(2):
            for j in range(4 * half, 4 * half + 4):
                for c in (8 + j, j, 16 + j):
                    wseq.append(w_in_r[l, c])
            for kk in range(4):
                wseq.append(w_out_r[l, 4 * half + kk])
        for half in range(2):
            for j in range(4 * half, 4 * half + 4):
                for c in (48 + j, 24 + j, 32 + j, 40 + j):
                    wseq.append(w_in_r[l, c])
            for kk in range(4):
                wseq.append(w_out_r[l, 8 + 4 * half + kk])
    wstate = {"loaded": 0, "next": 0}

    def w_prefetch(upto):
        while wstate["loaded"] < min(upto, len(wseq)):
            i = wstate["loaded"]
            s = i % NW
            P.op("pool", (lambda e, i=i, s=s: e.dma_start(out=wsl[:, s, :], in_=wseq[i])),
                 writes=[w_r[s]], chan=ch_w[s], name=f"wload{i}")
            wstate["loaded"] += 1

    def w_get_batch(k):
        start = wstate["next"]
        wstate["next"] += k
        w_prefetch(start + NW)
        return [(start + i) % NW for i in range(k)]

    def mm_group(bank, n, pairs, reads, writes_extra=(), out_view=None):
        outap = out_view if out_view is not None else banks[bank][:, 0:n]

        def fn(e):
            ins = None
            for i, (l_, r_) in enumerate(pairs):
                ins = e.matmul(outap, lhsT=l_, rhs=r_, start=(i == 0), stop=(i == len(pairs) - 1))
            return ins
        P.op("pe", fn, reads=reads, writes=[bank_r[bank]] + list(writes_extra))

    def blk_view(ap2d, b, n):
        if b == NB - 1:
            return ap2d.rearrange("p (s i) -> p s i", i=DS)
        return ap2d

    def rms_phase(gcol0, final=False):
        for b in range(NB):
            rms_block(gcol0, b, final)

    def rms_block(gcol0, b, final=False):
        rms_stats(b)
        rms_apply(gcol0, b, final)

    def rms_stats(b, pops=True):
        c0, n = BLOCKS[b]
        bk = pool_d.next()
        for kp in range(4):
            s = rot_sq.next()
            P.op("act", (lambda e, s=s, kp=kp, c0=c0, n=n: e.activation(
                out=sq[:, s, :, 0:n], in_=x_sb[:, 2 * kp:2 * kp + 2, c0:c0 + n], func=AF.Square)),
                reads=[x_r[2 * kp][b], x_r[2 * kp + 1][b]], writes=[sq_r[s]])

            def fn(e, s=s, kp=kp, n=n, bk=bk):
                ins = None
                for i in range(2):
                    ins = e.matmul(banks[bk][:, 0:n], lhsT=ones_s[:], rhs=sq[:, s, i, 0:n],
                                   start=(kp == 0 and i == 0), stop=(kp == 3 and i == 1))
                return ins
            P.op("pe", fn, reads=[sq_r[s], ones_r], writes=[bank_r[bk]])
            if pops:
                after_major()
        P.op("act", (lambda e, bk=bk, n=n: e.activation(out=lnv_t[:, 0, 0:n], in_=banks[bk][:, 0:n], func=AF.Ln, bias=RMS_EPS)),
             reads=[bank_r[bk]], writes=[lnv_r])
        P.op("act", (lambda e, n=n: e.activation(out=rstd_t[:, 0, 0:n], in_=lnv_t[:, 0, 0:n], func=AF.Exp, scale=-0.5)),
             reads=[lnv_r], writes=[rstd_r])

    def rms_apply(gcol0, b, final=False):
        c0, n = BLOCKS[b]
        for k in range(KC):
            if not final:
                P.op("dve", (lambda e, k=k, c0=c0, n=n: e.scalar_tensor_tensor(
                    out=h_sb[:, k, c0:c0 + n], in0=x_sb[:, k, c0:c0 + n], scalar=vecs[:, gcol0 + k:gcol0 + k + 1],
                    in1=rstd_t[:, 0, 0:n], op0=ALU.mult, op1=ALU.mult)),
                    reads=[x_r[k][b], rstd_r, const_r], writes=[h_r[k][b]])
            else:
                oi = (b * KC + k) % (NTH + NSZ)
                if oi < NTH:
                    tile_ap, tres = th_t[:, oi, 0:n], th_r[oi]
                else:
                    tile_ap, tres = sz_t[:, oi - NTH, 0:n], sz_r[oi - NTH]
                P.op("dve", (lambda e, k=k, c0=c0, n=n, tile_ap=tile_ap: e.scalar_tensor_tensor(
                    out=tile_ap, in0=x_sb[:, k, c0:c0 + n], scalar=vecs[:, gcol0 + k:gcol0 + k + 1],
                    in1=rstd_t[:, 0, 0:n], op0=ALU.mult, op1=ALU.mult)),
                    reads=[x_r[k][b], rstd_r, const_r], writes=[tres])
                P.op("sp", (lambda e, k=k, c0=c0, n=n, tile_ap=tile_ap: e.dma_start(out=y_fm[:, k, c0:c0 + n], in_=tile_ap)),
                     reads=[tres], chan=ch_out[oi])

    def out_round(l=0, rnd=0, after_block=None):
        slots = w_get_batch(4)
        st_fn, ap_fn = after_block if after_block is not None else (None, None)
        for b, (c0, n) in enumerate(BLOCKS):
            for m in range(KC):
                bk = pool_a.next()
                pairs = [(wsl[:, slots[kk], m * 128:(m + 1) * 128], y_sb[:, kk, c0:c0 + n]) for kk in range(4)]
                mm_group(bk, n, pairs, reads=[w_r[s] for s in slots] + [y_r[kk][b] for kk in range(4)])
                P.op("dve", (lambda e, bk=bk, m=m, c0=c0, n=n: e.tensor_tensor(
                    out=x_sb[:, m, c0:c0 + n], in0=x_sb[:, m, c0:c0 + n], in1=banks[bk][:, 0:n], op=ALU.add)),
                    reads=[bank_r[bk], x_r[m][b]], writes=[x_r[m][b]])
                if b * KC + m >= DRAIN_DELAY:
                    after_major()
                if ap_fn is not None and m == 3 and b >= 2:
                    ap_fn(b - 2)
            if st_fn is not None and b >= 1:
                st_fn(b - 1)
        if st_fn is not None:
            ap_fn(NB - 2)
            st_fn(NB - 1)
            ap_fn(NB - 1)
        if DEBUG:
            flush_deferred()
            for kk in range(4):
                P.op("sp", (lambda e, kk=kk: e.dma_start(out=dbg_y[:, l, 4 * rnd + kk, :], in_=y_sb[:, kk, SEQ:T])),
                     reads=[y_r[kk][NB - 1]], chan=ch_dbg)
            if rnd == 3:
                P.op("sp", (lambda e: e.dma_start(out=dbg_x[:, l], in_=x_sb[:, :, SEQ:T])),
                     reads=[x_r[m][NB - 1] for m in range(KC)], chan=ch_dbg)

    def a_head_prep(l, j):
        s = j % 2
        P.op("dve", (lambda e, s=s, l=l, j=j: e.tensor_tensor(
            out=mk[:, s, MK0:WA, :], in0=cmat[:, 0, :].unsqueeze(1).broadcast_to([128, WA - MK0, 128]),
            in1=cw_a[:, l, j, MK0:WA].unsqueeze(2).broadcast_to([128, WA - MK0, 128]), op=ALU.mult)),
            reads=[const_r], writes=[mk_r[s]])

    def a_hist(l, j):
        P.op("sp", (lambda e, l=l, j=j: e.dma_start(out=hista[:], in_=hist_a_d[:, l, j])), writes=[hista_r], chan=ch_hista)
        P.op("dve", (lambda e: e.tensor_scalar_mul(out=ubuf[:, US:US + (WA - 1) * NS], in0=hista[:], scalar1=2.0)),
             reads=[hista_r], writes=[uh_r])

    def a_step(l, j, b, jj, slots):
        c0, n = BLOCKS[b]
        sg, sv, sz_w = slots
        hreads = [h_r[k][b] for k in range(KC)]
        bg = pool_a.next()
        mm_group(bg, n, [(wsl[:, sg, k * 128:(k + 1) * 128], h_sb[:, k, c0:c0 + n]) for k in range(KC)], reads=[w_r[sg]] + hreads)
        pop_rank(1)
        ith = rot_th.next()
        P.op("act", (lambda e, bg=bg, ith=ith, n=n: e.activation(out=th_t[:, ith, 0:n], in_=banks[bg][:, 0:n], func=AF.Tanh, scale=0.5)),
             reads=[bank_r[bg]], writes=[th_r[ith]], name=f"th{j}.{b}")
        bv = pool_a.next()
        mm_group(bv, n, [(wsl[:, sv, k * 128:(k + 1) * 128], h_sb[:, k, c0:c0 + n]) for k in range(KC)], reads=[w_r[sv]] + hreads)
        if b < NB - 1:
            uout = ubuf[:, UH + c0:UH + c0 + n]
        else:
            uout = ubuf[:, US + (WA - 1) * NS:UW]
        P.op("dve", (lambda e, bv=bv, ith=ith, n=n, uout=uout, b=b: e.scalar_tensor_tensor(
            out=uout, in0=th_t[:, ith, 0:n], scalar=1.0, in1=banks[bv][:, 0:n],
            op0=ALU.add, op1=ALU.mult)),
            reads=[bank_r[bv], th_r[ith]], writes=[u_r[b]])
        if b == NB - 2:
            P.op("dve", (lambda e, bv=bv, ith=ith, n=n, j=j: e.scalar_tensor_tensor(
                out=oa_t[:, j, 0:WA - 1], in0=th_t[:, ith, n - (WA - 1):n], scalar=1.0, in1=banks[bv][:, n - (WA - 1):n],
                op0=ALU.add, op1=ALU.mult)),
                reads=[bank_r[bv], th_r[ith]], writes=[oa_r[j]])
        if b == NB - 1:
            P.op("dve", (lambda e, bv=bv, ith=ith, n=n, j=j: e.scalar_tensor_tensor(
                out=oa_t[:, j, WA - 1:WA - 1 + n], in0=th_t[:, ith, 0:n], scalar=1.0, in1=banks[bv][:, 0:n],
                op0=ALU.add, op1=ALU.mult)),
                reads=[bank_r[bv], th_r[ith]], writes=[oa_r[j]])
        pop_rank(2)
        bz = pool_a.next()
        mm_group(bz, n, [(wsl[:, sz_w, k * 128:(k + 1) * 128], h_sb[:, k, c0:c0 + n]) for k in range(KC)], reads=[w_r[sz_w]] + hreads)
        pop_rank(3)
        isz = rot_sz.next()
        P.op("act", (lambda e, bz=bz, isz=isz, n=n: e.activation(out=sz_t[:, isz, 0:n], in_=banks[bz][:, 0:n], func=AF.Silu)),
             reads=[bank_r[bz]], writes=[sz_r[isz]], name=f"sz{j}.{b}")
        pop_rank(0)
        bd = pool_d.next()
        ms = j % 2
        if b < NB - 1:
            def usl(k):
                return ubuf[:, c0 + k:c0 + k + n]
            creads = [u_r[b]] + ([u_r[b - 1]] if b > 0 else [pad_r])
        else:
            def usl(k):
                return ubuf[:, US + NS * k:US + NS * k + n]
            creads = [u_r[b], uh_r]
        nd = N_DVE_PRE_SAMPLE if b == NB - 2 else N_DVE
        pe_taps = list(range(nd, WA))

        def pe_fn(e):
            ins = None
            for i, k in enumerate(pe_taps):
                ins = e.matmul(banks[bd][:, 0:n], lhsT=mk[:, ms, k, :], rhs=usl(k), start=(i == 0), stop=(nd == 0 and i == len(pe_taps) - 1))
            return ins
        P.op("pe", pe_fn, reads=[mk_r[ms]] + creads, writes=[bank_r[bd]])
        iac = rot_acc.next()

        def dve_taps(k0, k1):
            chains = [[k for k in range(k0, k1) if k % 2 == 0], [k for k in range(k0, k1) if k % 2 == 1]]
            order = []
            for i in range(max(len(chains[0]), len(chains[1]))):
                for c in range(2):
                    if i < len(chains[c]):
                        order.append((c, i, chains[c][i]))
            for c, i, k in order:
                wcol = cwh[:, l, j, k:k + 1]
                last = (i == len(chains[c]) - 1)
                outap = accb[:, 2 * iac + c, 0:n] if last else acc[:, c, 0:n]
                wres = accb_r[2 * iac + c] if last else acc_r[c]
                if i == 0:
                    P.op("dve", (lambda e, k=k, wcol=wcol, outap=outap: e.tensor_scalar_mul(out=outap, in0=usl(k), scalar1=wcol)),
                         reads=creads + [cwh_r], writes=[wres])
                else:
                    P.op("dve", (lambda e, k=k, wcol=wcol, outap=outap, c=c: e.scalar_tensor_tensor(
                        out=outap, in0=usl(k), scalar=wcol, in1=acc[:, c, 0:n], op0=ALU.mult, op1=ALU.add)),
                        reads=creads + [cwh_r, acc_r[c]], writes=[wres])
            return [c for c in range(2) if chains[c]]
        used_chains = dve_taps(0, nd)
        pop_rank(4)
        id2 = rot_d2.next()
        cbc = cbias[:, l * 8 + j:l * 8 + j + 1]
        bvar_box = {}

        def stage0():
            for ci, c in enumerate(used_chains):
                P.op("pe", (lambda e, c=c, ci=ci: e.matmul(banks[bd][:, 0:n], lhsT=cb16[:], rhs=accb[:, 2 * iac + c, 0:n],
                                                        start=False, stop=(ci == len(used_chains) - 1))),
                     reads=[accb_r[2 * iac + c], cb16_r], writes=[bank_r[bd]])
            P.op("act", (lambda e: e.activation(out=d2[:, id2, 0:n], in_=banks[bd][:, 0:n], func=AF.Square, bias=cbc)),
                 reads=[bank_r[bd], cbias_r], writes=[d2_r[id2]], name=f"d2_{j}.{b}")

        def stage1():
            bvv = pool_a.next()
            bvar_box["b"] = bvv
            mm_group(bvv, n, [(ones_v[:], d2[:, id2, 0:n])], reads=[d2_r[id2], ones_r])

        def stage2():
            bvv = bvar_box["b"]
            P.op("act", (lambda e: e.activation(out=lnv_t[:, 0, 0:n], in_=banks[bvv][:, 0:n], func=AF.Ln, bias=LN_EPS)),
                 reads=[bank_r[bvv]], writes=[lnv_r], name=f"ln{j}.{b}")
            P.op("act", (lambda e: e.activation(out=rstd_t[:, 0, 0:n], in_=lnv_t[:, 0, 0:n], func=AF.Exp, scale=-0.5)),
                 reads=[lnv_r], writes=[rstd_r], name=f"exp{j}.{b}")
            P.op("dve", (lambda e: e.scalar_tensor_tensor(out=tt_t[:, 0, 0:n], in0=banks[bd][:, 0:n], scalar=cbc, in1=rstd_t[:, 0, 0:n],
                                                            op0=ALU.add, op1=ALU.mult)),
                 reads=[bank_r[bd], rstd_r, cbias_r], writes=[tt_r])

        def stage3():
            gcol = vecs[:, 32 + l * 8 + j:32 + l * 8 + j + 1]
            bcol = vecs[:, 48 + l * 8 + j:48 + l * 8 + j + 1]
            P.op("act", (lambda e: e.activation(out=ss_t[:, 0, 0:n], in_=tt_t[:, 0, 0:n], func=AF.Silu, scale=gcol, bias=bcol)),
                 reads=[tt_r, const_r], writes=[ss_r], name=f"s{j}.{b}")

        def stage4():
            P.op("dve", (lambda e: e.tensor_tensor(out=y_sb[:, jj, c0:c0 + n], in0=ss_t[:, 0, 0:n], in1=sz_t[:, isz, 0:n], op=ALU.mult)),
                 reads=[ss_r, sz_r[isz]], writes=[y_r[jj][b]])
        sid = step_ctr[0]
        step_ctr[0] += 1
        for rk, fn in enumerate([stage0, stage1, stage2, stage3, stage4]):
            defer(sid, rk, fn)

    def a_finish(l):
        P.op("dve", (lambda e: e.tensor_scalar_mul(out=oa_t[:], in0=oa_t[:], scalar1=0.5)),
             reads=oa_r, writes=oa_r)
        P.op("sp", (lambda e, l=l: e.dma_start(out=oa_p[:, l], in_=oa_t[:, :, 0:WA - 1])), reads=oa_r, chan=ch_oa)
        P.op("sp", (lambda e, l=l: e.dma_start(out=oa_new[:, l], in_=oa_t[:, :, WA - 1:WA - 1 + NS * DS])), reads=oa_r, chan=ch_oa)

    def b_group_prep(l, j):
        s = j % 2
        P.op("dve", (lambda e, s=s, l=l, j=j: e.tensor_tensor(
            out=dk[:, s, :, :], in0=cmat[:, 1, :].unsqueeze(1).broadcast_to([128, WB, 128]),
            in1=cw_b[:, l, j, :].unsqueeze(2).broadcast_to([128, WB, 128]), op=ALU.mult)),
            reads=[const_r], writes=[dk_r[s]])

    def b_hist(l, j):
        P.op("dve", (lambda e, l=l, j=j: e.tensor_copy(vbuf[:, VS:VS + (WB - 1) * NS], histb[:, l, j, :])),
             reads=[histb_r], writes=[vh_r])

    def b_step(l, j, b, jj, slots):
        c0, n = BLOCKS[b]
        s_zb, s_gb, s_gc, s_hb = slots
        hreads = [h_r[k][b] for k in range(KC)]

        def inproj(slot):
            bk = pool_a.next()
            mm_group(bk, n, [(wsl[:, slot, k * 128:(k + 1) * 128], h_sb[:, k, c0:c0 + n]) for k in range(KC)], reads=[w_r[slot]] + hreads)
            after_major()
            return bk
        bzb = inproj(s_zb)
        if b == 0:
            b_hist(l, j)
        isz = rot_sz.next()
        P.op("act", (lambda e: e.activation(out=sz_t[:, isz, 0:n], in_=banks[bzb][:, 0:n], func=AF.Silu)),
             reads=[bank_r[bzb]], writes=[sz_r[isz]])
        bgb = inproj(s_gb)
        igz = rot_gz.next()
        P.op("dve", (lambda e: e.tensor_tensor(out=gz_t[:, igz, 0:n], in0=banks[bgb][:, 0:n], in1=sz_t[:, isz, 0:n], op=ALU.mult)),
             reads=[bank_r[bgb], sz_r[isz]], writes=[gz_r[igz]])
        bgc = inproj(s_gc)
        P.op("act", (lambda e: e.activation(out=gcs_t[:, 0, 0:n], in_=banks[bgc][:, 0:n], func=AF.Copy)),
             reads=[bank_r[bgc]], writes=[gcs_r])
        bhb = inproj(s_hb)
        if b < NB - 1:
            vout = vbuf[:, VH + c0:VH + c0 + n]
        else:
            vout = vbuf[:, VS + (WB - 1) * NS:VW]
        P.op("dve", (lambda e: e.tensor_tensor(out=vout, in0=banks[bhb][:, 0:n], in1=gcs_t[:, 0, 0:n], op=ALU.mult)),
             reads=[bank_r[bhb], gcs_r], writes=[v_r[b]])
        if b == NB - 2:
            P.op("dve", (lambda e: e.tensor_tensor(out=ob_t[:, j, 0:WB - 1], in0=banks[bhb][:, n - (WB - 1):n], in1=gcs_t[:, 0, n - (WB - 1):n], op=ALU.mult)),
                 reads=[bank_r[bhb], gcs_r], writes=[ob_r[j]])
        if b == NB - 1:
            P.op("dve", (lambda e: e.tensor_tensor(out=ob_t[:, j, WB - 1:WB - 1 + n], in0=banks[bhb][:, 0:n], in1=gcs_t[:, 0, 0:n], op=ALU.mult)),
                 reads=[bank_r[bhb], gcs_r], writes=[ob_r[j]])

        if b < NB - 1:
            def vsl(k):
                return vbuf[:, c0 + k:c0 + k + n]
            creads = [v_r[b]] + ([v_r[b - 1]] if b > 0 else [pad_r])
        else:
            def vsl(k):
                return vbuf[:, VS + NS * k:VS + NS * k + n]
            creads = [v_r[b], vh_r]
        iac = rot_acc.next()
        for k in range(WB):
            wcol = cw_b[:, l, j, k:k + 1]
            if k == 0:
                P.op("dve", (lambda e, k=k, wcol=wcol: e.tensor_scalar_mul(out=acc[:, iac, 0:n], in0=vsl(k), scalar1=wcol)),
                     reads=creads + [const_r], writes=[acc_r[iac]], relaxed=(acc_r[iac],))
            else:
                P.op("dve", (lambda e, k=k, wcol=wcol: e.scalar_tensor_tensor(
                    out=acc[:, iac, 0:n], in0=vsl(k), scalar=wcol, in1=acc[:, iac, 0:n], op0=ALU.mult, op1=ALU.add)),
                    reads=creads + [const_r, acc_r[iac]], writes=[acc_r[iac]], relaxed=(acc_r[iac],))
        P.op("dve", (lambda e: e.tensor_tensor(out=y_sb[:, jj, c0:c0 + n], in0=acc[:, iac, 0:n], in1=gz_t[:, igz, 0:n], op=ALU.mult)),
             reads=[acc_r[iac], gz_r[igz]], writes=[y_r[jj][b]], relaxed=(acc_r[iac],))

    def b_finish(l):
        P.op("sp", (lambda e, l=l: e.dma_start(out=ob_p[:, l], in_=ob_t[:, :, 0:WB - 1])), reads=ob_r, chan=ch_ob)
        P.op("sp", (lambda e, l=l: e.dma_start(out=ob_new[:, l], in_=ob_t[:, :, WB - 1:WB - 1 + NS * DS])), reads=ob_r, chan=ch_ob)

    w_prefetch(NW)
    for l in range(DEPTH):
        if l == 0:
            rms_phase(0)
            a_head_prep(l, 0)
        for half in range(2):
            for j in range(4 * half, 4 * half + 4):
                slots = w_get_batch(3)
                if j + 1 < 8:
                    a_head_prep(l, j + 1)
                a_hist(l, j)
                for b in range(NB):
                    a_step(l, j, b, j - 4 * half, slots)
            out_round(l, half)
            flush_deferred()
        a_finish(l)
        for half in range(2):
            for j in range(4 * half, 4 * half + 4):
                slots = w_get_batch(4)
                for b in range(NB):
                    b_step(l, j, b, j - 4 * half, slots)
            nxt = None
            if half == 1 and l + 1 < DEPTH:
                a_head_prep(l + 1, 0)
            if half == 1:
                if l + 1 < DEPTH:
                    nxt = ((lambda b: rms_stats(b, pops=False)), (lambda b, l=l: rms_apply((l + 1) * 8, b)))
                else:
                    nxt = ((lambda b: rms_stats(b, pops=False)), (lambda b: rms_apply(64, b, final=True)))
            out_round(l, 2 + half, after_block=nxt)
            flush_deferred()
        b_finish(l)

    final_waits = ch_out + [ch_oa, ch_ob, ch_copy] + ([ch_dbg] if DEBUG else [])
    sem_ctx = {}
    for e in Prog.ENGS:
        sem_ctx[e] = es.enter_context(nc.semaphore(f"s_{e}"))
    chan_sems = {c: es.enter_context(nc.semaphore(f"c_{c}")) for c in chans}
    block = es.enter_context(nc.Block())
    P.emit(nc, block, sem_ctx, chan_sems, final_waits)
    es.close()
    return nc


_NC_CACHE = {}


def _get_nc():
    if "nc" not in _NC_CACHE:
        _NC_CACHE["nc"] = build_nc()
    return _NC_CACHE["nc"]


def _prep_shared(norm_g, w_in, conv_a_w, conv_a_b, ln_a_g, ln_a_b, conv_b_w, w_out, final_g):
    f = np.float32
    w_in_r = np.ascontiguousarray(
        np.asarray(w_in, f).reshape(DEPTH, KC, 128, 56, 128).transpose(0, 3, 2, 1, 4)).reshape(DEPTH, 56, 128, 1024)
    w_out_r = np.ascontiguousarray(np.asarray(w_out, f).reshape(DEPTH, 16, 128, 1024))
    cw_a = np.ascontiguousarray(np.asarray(conv_a_w, f).reshape(DEPTH, WA, 8, 128).transpose(3, 0, 2, 1))
    cw_b = np.ascontiguousarray(np.asarray(conv_b_w, f).reshape(DEPTH, WB, 8, 128).transpose(3, 0, 2, 1))

    def pv(v):
        return np.asarray(v, f).reshape(-1, 8, 128).transpose(2, 0, 1).reshape(128, -1)
    vecs = np.ascontiguousarray(np.concatenate(
        [pv(norm_g), pv(conv_a_b), pv(ln_a_g), pv(ln_a_b), pv(np.asarray(final_g, f)[None])], axis=1))
    cmat = np.zeros((128, 2, 128), f)
    cmat[:, 0, :] = 0.5 * (np.eye(128, dtype=f) - f(1.0 / 128))
    cmat[:, 1, :] = np.eye(128, dtype=f)
    return dict(w_in_r=w_in_r, w_out_r=w_out_r, cw_a=cw_a, cw_b=cw_b, vecs=vecs, cmat=cmat)


def kernel(x_prompt, x_sample, state_conv_a, state_conv_b, norm_g, w_in, conv_a_w, conv_a_b,
           ln_a_g, ln_a_b, conv_b_w, w_out, final_g):
    f = np.float32
    nc = _get_nc()
    shared = _prep_shared(norm_g, w_in, conv_a_w, conv_a_b, ln_a_g, ln_a_b, conv_b_w, w_out, final_g)
    x_prompt = np.asarray(x_prompt, f)
    x_sample = np.asarray(x_sample, f)
    state_conv_a = np.asarray(state_conv_a, f)
    state_conv_b = np.asarray(state_conv_b, f)
    in_maps = []
    for c in range(N_CORES):
        xs = x_sample[NS * c:NS * (c + 1)].transpose(1, 0, 2).reshape(NS * DS, D)
        xt = np.concatenate([x_prompt[c], xs], axis=0)
        xT = np.ascontiguousarray(xt.reshape(T, KC, 128).transpose(2, 1, 0))
        sa = state_conv_a[:, NS * c:NS * (c + 1)]
        sbb = state_conv_b[:, NS * c:NS * (c + 1)]
        hist_a = np.ascontiguousarray(sa.reshape(DEPTH, NS, WA - 1, 8, 128).transpose(4, 0, 3, 2, 1)).reshape(128, DEPTH, 8, (WA - 1) * NS)
        hist_b = np.ascontiguousarray(sbb.reshape(DEPTH, NS, WB - 1, 8, 128).transpose(4, 0, 3, 2, 1)).reshape(128, DEPTH, 8, (WB - 1) * NS)
        m = dict(shared)
        m.update(xT=xT, hist_a=hist_a, hist_b=hist_b, sa_orig=np.ascontiguousarray(sa))
        in_maps.append(m)
    res = run_bass_kernel_spmd(nc, in_maps, core_ids=list(range(N_CORES)))
    y_prompt = np.empty((N_CORES, SEQ, D), f)
    y_sample = np.empty((N_CORES * NS, DS, D), f)
    na_p = np.empty((DEPTH, N_CORES, WA - 1, D), f)
    nb_p = np.empty((DEPTH, N_CORES, WB - 1, D), f)
    na_s = np.empty((DEPTH, N_CORES * NS, WA - 1, D), f)
    nb_s = np.empty((DEPTH, N_CORES * NS, WB - 1, D), f)
    for c in range(N_CORES):
        r = res.results[c]
        yfm = np.asarray(r["y_fm"], f)
        ytm = yfm.transpose(2, 1, 0).reshape(T, D)
        y_prompt[c] = ytm[:SEQ]
        y_sample[NS * c:NS * (c + 1)] = ytm[SEQ:].reshape(DS, NS, D).transpose(1, 0, 2)
        oap = np.asarray(r["oa_p"], f)
        na_p[:, c] = oap.transpose(1, 3, 2, 0).reshape(DEPTH, WA - 1, D)
        obp = np.asarray(r["ob_p"], f)
        nb_p[:, c] = obp.transpose(1, 3, 2, 0).reshape(DEPTH, WB - 1, D)
        na_s[:, NS * c:NS * (c + 1), :WA - 1 - DS] = np.asarray(r["oa_copy"], f)
        oan = np.asarray(r["oa_new"], f).reshape(128, DEPTH, 8, DS, NS)
        na_s[:, NS * c:NS * (c + 1), WA - 1 - DS:] = oan.transpose(1, 4, 3, 2, 0).reshape(DEPTH, NS, DS, D)
        obn = np.asarray(r["ob_new"], f).reshape(128, DEPTH, 8, DS, NS)
        nb_s[:, NS * c:NS * (c + 1)] = obn.transpose(1, 4, 3, 2, 0).reshape(DEPTH, NS, DS, D)[:, :, DS - (WB - 1):]
    if DEBUG:
        _NC_CACHE["dbg"] = [(np.asarray(r["dbg_y"]).astype(f), np.asarray(r["dbg_x"], f)) for r in res.results]
    return (y_prompt, y_sample, na_p, nb_p, na_s, nb_s)
```

```python
from contextlib import ExitStack
import numpy as np
import concourse.bass as bass
import concourse.mybir as mybir
from concourse.bass_utils import run_bass_kernel_spmd

F32 = mybir.dt.float32
BF16 = mybir.dt.bfloat16
AF = mybir.ActivationFunctionType
ALU = mybir.AluOpType

N_CORES = 8
D = 1024
DEPTH = 2
SEQ = 2048
NS = 16
DS = 4
T = SEQ + NS * DS
KC = 8
WA = 31
WB = 3
RMS_EPS = 1e-6
LN_EPS = 1e-5
BLOCKS = [(0, 512), (512, 512), (1024, 512), (1536, 512), (2048, 64)]
NB = len(BLOCKS)
NW = 9
UH = 30
US = UH + SEQ
UW = US + NS * (WA - 1 + DS)
VH = 2
VS = VH + SEQ
VW = VS + NS * (WB - 1 + DS)
K_SAME = 3
DRAIN_DELAY = 6
RELAX_INPLACE = False
N_DVE = 12
N_DVE_PRE_SAMPLE = 12


class Res:
    __slots__ = ("name", "w", "r")

    def __init__(self, name):
        self.name = name
        self.w = None
        self.r = []


class Op:
    __slots__ = ("eng", "fn", "deps", "idx", "sig", "cnt", "chan", "name")


class Prog:
    ENGS = ("pe", "act", "dve", "pool", "sp")

    def __init__(self):
        self.ops = {e: [] for e in self.ENGS}
        self.chan_ops = {}

    def op(self, eng, fn, reads=(), writes=(), chan=None, name="", relaxed=()):
        if not RELAX_INPLACE:
            relaxed = ()
        o = Op()
        o.eng, o.fn, o.chan, o.name = eng, fn, chan, name
        o.sig = False
        o.cnt = 0
        o.idx = len(self.ops[eng])
        deps = set()
        for r in reads:
            if r.w is not None and not (r in relaxed and r.w.eng == eng and r.w.chan is None):
                deps.add(r.w)
        for w in writes:
            rel = w in relaxed
            if w.w is not None and not (rel and w.w.eng == eng and w.w.chan is None):
                deps.add(w.w)
            for x in w.r:
                if not (rel and x.eng == eng and x.chan is None):
                    deps.add(x)
        deps.discard(o)
        keep = []
        for d in deps:
            if d.chan is None and chan is None and d.eng == eng:
                if eng == "pe":
                    continue
                if o.idx - d.idx > K_SAME:
                    continue
            keep.append(d)
            d.sig = True
        o.deps = keep
        for r in reads:
            r.r.append(o)
        for w in writes:
            w.w = o
            w.r = []
        self.ops[eng].append(o)
        if chan is not None:
            self.chan_ops.setdefault(chan, []).append(o)
        return o

    def emit(self, nc, block, sems, chan_sems, final_waits):
        for e in self.ENGS:
            c = 0
            for o in self.ops[e]:
                if o.chan is None and o.sig:
                    c += 1
                    o.cnt = c
        for ch, lst in self.chan_ops.items():
            c = 0
            for o in lst:
                c += 16
                o.cnt = c
        chan_total = {ch: 16 * len(lst) for ch, lst in self.chan_ops.items()}

        def run(eng_name):
            def body(e):
                waited = {}
                for o in self.ops[eng_name]:
                    need = {}
                    for d in o.deps:
                        s = chan_sems[d.chan] if d.chan is not None else sems[d.eng]
                        if need.get(s.num, (None, 0))[1] < d.cnt:
                            need[s.num] = (s, d.cnt)
                    for num, (s, c) in need.items():
                        if waited.get(num, 0) < c:
                            e.wait_ge(s, c)
                            waited[num] = c
                    ins = o.fn(e)
                    if o.chan is not None:
                        ins.then_inc(chan_sems[o.chan], 16)
                    elif o.sig:
                        ins.then_inc(sems[eng_name], 1)
                if eng_name == "sp":
                    for ch in final_waits:
                        e.wait_ge(chan_sems[ch], chan_total[ch])
            return body

        block.tensor(run("pe"))
        block.scalar(run("act"))
        block.vector(run("dve"))
        block.gpsimd(run("pool"))
        block.sync(run("sp"))


DEBUG = False


def build_nc():
    nc = bass.Bass("TRN2", target_bir_lowering=False)
    P = Prog()
    es = ExitStack()

    def din(name, shape, dt=F32):
        return nc.dram_tensor(name, list(shape), dt, kind="ExternalInput").ap()

    def dout(name, shape, dt=F32):
        return nc.dram_tensor(name, list(shape), dt, kind="ExternalOutput").ap()

    xT = din("xT", [128, KC, T])
    w_in_r = din("w_in_r", [DEPTH, 56, 128, 1024])
    w_out_r = din("w_out_r", [DEPTH, 16, 128, 1024])
    cw_a_d = din("cw_a", [128, DEPTH, 8, WA])
    cw_b_d = din("cw_b", [128, DEPTH, 8, WB])
    vecs_d = din("vecs", [128, 72])
    hist_a_d = din("hist_a", [128, DEPTH, 8, (WA - 1) * NS])
    hist_b_d = din("hist_b", [128, DEPTH, 8, (WB - 1) * NS])
    sa_orig = din("sa_orig", [DEPTH, NS, WA - 1, D])
    cmat_d = din("cmat", [128, 2, 128])

    y_fm = dout("y_fm", [128, KC, T])
    oa_p = dout("oa_p", [128, DEPTH, 8, WA - 1])
    ob_p = dout("ob_p", [128, DEPTH, 8, WB - 1])
    oa_copy = dout("oa_copy", [DEPTH, NS, WA - 1 - DS, D])
    oa_new = dout("oa_new", [128, DEPTH, 8, NS * DS])
    ob_new = dout("ob_new", [128, DEPTH, 8, NS * DS])

    if DEBUG:
        dbg_y = dout("dbg_y", [128, DEPTH, 16, NS * DS], BF16)
        dbg_x = dout("dbg_x", [128, DEPTH, KC, NS * DS], F32)

    def sb(name, shape, dt):
        return es.enter_context(nc.sbuf_tensor(name, list(shape), dt))

    def ps(name):
        return es.enter_context(nc.psum_tensor(name, [128, 512], F32))

    x_sb = sb("x_sb", [128, KC, T], F32)
    h_sb = sb("h_sb", [128, KC, T], BF16)
    y_sb = sb("y_sb", [128, 4, T], BF16)
    ubuf = sb("ubuf", [128, UW], BF16)
    vbuf = sb("vbuf", [128, VW], BF16)
    wsl = sb("wsl", [128, NW, 1024], BF16)
    mk = sb("mk", [128, 2, WA, 128], BF16)
    dk = sb("dk", [128, 2, WB, 128], BF16)
    sq = sb("sq", [128, 2, 2, 512], BF16)
    d2 = sb("d2", [128, 2, 512], BF16)
    NTH, NSZ, NGZ = 2, 3, 2
    th_t = sb("th_t", [128, NTH, 512], F32)
    sz_t = sb("sz_t", [128, NSZ, 512], F32)
    gz_t = th_t
    lnv_t = sb("lnv_t", [128, 1, 512], F32)
    rstd_t = sb("rstd_t", [128, 1, 512], F32)
    tt_t = sb("tt_t", [128, 1, 512], F32)
    ss_t = sb("ss_t", [128, 1, 512], F32)
    gcs_t = tt_t
    acc = sb("acc", [128, 2, 512], F32)
    accb = sb("accb", [128, 4, 512], BF16)
    cwh = sb("cwh", [128, DEPTH, 8, WA], F32)
    cb16 = sb("cb16", [128, 128], BF16)
    oa_t = sb("oa_t", [128, 8, (WA - 1) + NS * DS], F32)
    ob_t = sb("ob_t", [128, 8, (WB - 1) + NS * DS], F32)
    hista = sb("hista", [128, (WA - 1) * NS], F32)
    histb = sb("histb", [128, DEPTH, 8, (WB - 1) * NS], F32)
    cmat = sb("cmat_sb", [128, 2, 128], F32)
    ones_s = sb("ones_s", [128, 128], BF16)
    ones_v = sb("ones_v", [128, 128], BF16)
    vecs = sb("vecs_sb", [128, 72], F32)
    cbias = sb("cbias", [128, 16], F32)
    cw_a = sb("cw_a_sb", [128, DEPTH, 8, WA], F32)
    cw_b = sb("cw_b_sb", [128, DEPTH, 8, WB], F32)
    banks = [ps(f"psb{i}") for i in range(8)]

    R = Res
    x_r = [[R(f"x{k}_{b}") for b in range(NB)] for k in range(KC)]
    h_r = [[R(f"h{k}_{b}") for b in range(NB)] for k in range(KC)]
    y_r = [[R(f"y{k}_{b}") for b in range(NB)] for k in range(4)]
    u_r = [R(f"u{b}") for b in range(NB)]
    uh_r = R("uhist")
    v_r = [R(f"v{b}") for b in range(NB)]
    vh_r = R("vhist")
    pad_r = R("pads")
    w_r = [R(f"w{i}") for i in range(NW)]
    mk_r = [R("mk0"), R("mk1")]
    dk_r = [R("dk0"), R("dk1")]
    sq_r = [R("sq0"), R("sq1")]
    d2_r = [R("d20"), R("d21")]
    th_r = [R(f"th{i}") for i in range(NTH)]
    sz_r = [R(f"sz{i}") for i in range(NSZ)]
    gz_r = th_r
    lnv_r, rstd_r, tt_r, ss_r = R("lnv"), R("rstd"), R("tt"), R("ss")
    gcs_r = tt_r
    acc_r = [R("acc0"), R("acc1")]
    accb_r = [R(f"accb{i}") for i in range(4)]
    cwh_r = R("cwh")
    cb16_r = R("cb16")
    oa_r = [R(f"oa{j}") for j in range(8)]
    ob_r = [R(f"ob{j}") for j in range(8)]
    hista_r = R("hista")
    histb_r = R("histb")
    const_r = R("consts")
    cbias_r = R("cbias")
    bank_r = [R(f"bank{i}") for i in range(8)]

    class Rot:
        def __init__(self, ids):
            self.ids = list(ids)
            self.i = 0

        def next(self):
            v = self.ids[self.i % len(self.ids)]
            self.i += 1
            return v

    pool_a = Rot([0, 1, 2, 3])
    pool_d = Rot([4, 5, 6, 7])
    rot_th, rot_sz, rot_gz, rot_sq, rot_d2, rot_acc = Rot(range(NTH)), Rot(range(NSZ)), Rot(range(NGZ)), Rot(range(2)), Rot(range(2)), Rot(range(2))

    chans = []

    def chan(name):
        chans.append(name)
        return name

    ch_c = [chan(f"const{i}") for i in range(5)]
    ch_x = [chan(f"x{b}") for b in range(NB)]
    ch_w = [chan(f"w{i}") for i in range(NW)]
    ch_hista = chan("hista")
    ch_out = [chan(f"out{i}") for i in range(NTH + NSZ)]
    ch_oa = chan("oa")
    ch_ob = chan("ob")
    ch_copy = chan("copy")
    ch_dbg = chan("dbg")

    deferred = []
    step_ctr = [0]

    def defer(step_id, rank, fn):
        deferred.append((step_id, rank, fn))

    def after_major():
        if deferred:
            i = min(range(len(deferred)), key=lambda i: deferred[i][:2])
            deferred.pop(i)[2]()

    def pop_rank(rank):
        cand = [i for i in range(len(deferred)) if deferred[i][1] == rank
                and not any(d[0] == deferred[i][0] and d[1] < rank for d in deferred)]
        if cand:
            i = min(cand, key=lambda i: deferred[i][0])
            deferred.pop(i)[2]()

    def flush_deferred():
        while deferred:
            after_major()

    P.op("sp", lambda e: e.dma_start(out=vecs[:], in_=vecs_d), writes=[const_r], chan=ch_c[0])
    P.op("sp", lambda e: e.dma_start(out=cmat[:], in_=cmat_d), writes=[const_r], chan=ch_c[1])
    P.op("sp", lambda e: e.dma_start(out=cw_a[:], in_=cw_a_d), writes=[const_r], chan=ch_c[2])
    P.op("sp", lambda e: e.dma_start(out=cw_b[:], in_=cw_b_d), writes=[const_r], chan=ch_c[3])
    P.op("sp", lambda e: e.dma_start(out=histb[:], in_=hist_b_d), writes=[histb_r], chan=ch_c[4])
    for b, (c0, n) in enumerate(BLOCKS):
        P.op("sp", (lambda e, c0=c0, n=n: e.dma_start(out=x_sb[:, :, c0:c0 + n], in_=xT[:, :, c0:c0 + n])),
             writes=[x_r[k][b] for k in range(KC)], chan=ch_x[b])
    for l in range(DEPTH):
        P.op("sp", (lambda e, l=l: e.dma_start(out=oa_copy[l], in_=sa_orig[l, :, DS:WA - 1, :])), chan=ch_copy)
    ones_r = R("ones")
    P.op("dve", lambda e: e.memset(ones_s[:], 1.0 / D), writes=[ones_r])
    P.op("dve", lambda e: e.memset(ones_v[:], 1.0 / 128), writes=[ones_r])
    P.op("dve", lambda e: e.memset(ubuf[:, 0:UH], 0.0), writes=[pad_r])
    P.op("dve", lambda e: e.memset(vbuf[:, 0:VH], 0.0), writes=[pad_r])
    P.op("dve", lambda e: e.tensor_scalar_mul(out=cwh[:], in0=cw_a[:], scalar1=0.5), reads=[const_r], writes=[cwh_r])
    P.op("dve", lambda e: e.tensor_scalar_mul(out=cb16[:], in0=cmat[:, 0, :], scalar1=2.0), reads=[const_r], writes=[cb16_r])
    bk = pool_a.next()
    P.op("pe", (lambda e, bk=bk: e.matmul(banks[bk][:, 0:16], lhsT=cmat[:, 0, :], rhs=vecs[:, 16:32], start=True, stop=True)),
         reads=[const_r], writes=[bank_r[bk]])
    P.op("act", (lambda e, bk=bk: e.activation(out=cbias[:], in_=banks[bk][:, 0:16], func=AF.Copy, scale=2.0)),
         reads=[bank_r[bk]], writes=[cbias_r])

    wseq = []
    for l in range(DEPTH):
        for half in range(2):
            for j in range(4 * half, 4 * half + 4):
                for c in (8 + j, j, 16 + j):
                    wseq.append(w_in_r[l, c])
            for kk in range(4):
                wseq.append(w_out_r[l, 4 * half + kk])
        for half in range(2):
            for j in range(4 * half, 4 * half + 4):
                for c in (48 + j, 24 + j, 32 + j, 40 + j):
                    wseq.append(w_in_r[l, c])
            for kk in range(4):
                wseq.append(w_out_r[l, 8 + 4 * half + kk])
    wstate = {"loaded": 0, "next": 0}

    def w_prefetch(upto):
        while wstate["loaded"] < min(upto, len(wseq)):
            i = wstate["loaded"]
            s = i % NW
            P.op("pool", (lambda e, i=i, s=s: e.dma_start(out=wsl[:, s, :], in_=wseq[i])),
                 reads=([x_r[0][0]] if i == 0 else []),
                 writes=[w_r[s]], chan=ch_w[s], name=f"wload{i}")
            wstate["loaded"] += 1

    def w_get_batch(k):
        start = wstate["next"]
        wstate["next"] += k
        w_prefetch(start + NW)
        return [(start + i) % NW for i in range(k)]

    def mm_group(bank, n, pairs, reads, writes_extra=(), out_view=None):
        outap = out_view if out_view is not None else banks[bank][:, 0:n]

        def fn(e):
            ins = None
            for i, (l_, r_) in enumerate(pairs):
                ins = e.matmul(outap, lhsT=l_, rhs=r_, start=(i == 0), stop=(i == len(pairs) - 1))
            return ins
        P.op("pe", fn, reads=reads, writes=[bank_r[bank]] + list(writes_extra))

    def blk_view(ap2d, b, n):
        if b == NB - 1:
            return ap2d.rearrange("p (s i) -> p s i", i=DS)
        return ap2d

    def rms_phase(gcol0, final=False):
        for b in range(NB):
            rms_block(gcol0, b, final)

    def rms_block(gcol0, b, final=False):
        rms_stats(b)
        rms_apply(gcol0, b, final)

    def rms_stats(b, pops=True):
        c0, n = BLOCKS[b]
        bk = pool_d.next()
        for kp in range(4):
            s = rot_sq.next()
            P.op("act", (lambda e, s=s, kp=kp, c0=c0, n=n: e.activation(
                out=sq[:, s, :, 0:n], in_=x_sb[:, 2 * kp:2 * kp + 2, c0:c0 + n], func=AF.Square)),
                reads=[x_r[2 * kp][b], x_r[2 * kp + 1][b]], writes=[sq_r[s]])

            def fn(e, s=s, kp=kp, n=n, bk=bk):
                ins = None
                for i in range(2):
                    ins = e.matmul(banks[bk][:, 0:n], lhsT=ones_s[:], rhs=sq[:, s, i, 0:n],
                                   start=(kp == 0 and i == 0), stop=(kp == 3 and i == 1))
                return ins
            P.op("pe", fn, reads=[sq_r[s], ones_r], writes=[bank_r[bk]])
            if pops:
                after_major()
        P.op("act", (lambda e, bk=bk, n=n: e.activation(out=lnv_t[:, 0, 0:n], in_=banks[bk][:, 0:n], func=AF.Ln, bias=RMS_EPS)),
             reads=[bank_r[bk]], writes=[lnv_r])
        P.op("act", (lambda e, n=n: e.activation(out=rstd_t[:, 0, 0:n], in_=lnv_t[:, 0, 0:n], func=AF.Exp, scale=-0.5)),
             reads=[lnv_r], writes=[rstd_r])

    def rms_apply(gcol0, b, final=False):
        c0, n = BLOCKS[b]
        for k in range(KC):
            if not final:
                P.op("dve", (lambda e, k=k, c0=c0, n=n: e.scalar_tensor_tensor(
                    out=h_sb[:, k, c0:c0 + n], in0=x_sb[:, k, c0:c0 + n], scalar=vecs[:, gcol0 + k:gcol0 + k + 1],
                    in1=rstd_t[:, 0, 0:n], op0=ALU.mult, op1=ALU.mult)),
                    reads=[x_r[k][b], rstd_r, const_r], writes=[h_r[k][b]])
            else:
                oi = (b * KC + k) % (NTH + NSZ)
                if oi < NTH:
                    tile_ap, tres = th_t[:, oi, 0:n], th_r[oi]
                else:
                    tile_ap, tres = sz_t[:, oi - NTH, 0:n], sz_r[oi - NTH]
                P.op("dve", (lambda e, k=k, c0=c0, n=n, tile_ap=tile_ap: e.scalar_tensor_tensor(
                    out=tile_ap, in0=x_sb[:, k, c0:c0 + n], scalar=vecs[:, gcol0 + k:gcol0 + k + 1],
                    in1=rstd_t[:, 0, 0:n], op0=ALU.mult, op1=ALU.mult)),
                    reads=[x_r[k][b], rstd_r, const_r], writes=[tres])
                P.op("sp", (lambda e, k=k, c0=c0, n=n, tile_ap=tile_ap: e.dma_start(out=y_fm[:, k, c0:c0 + n], in_=tile_ap)),
                     reads=[tres], chan=ch_out[oi])

    def out_round(l=0, rnd=0, after_block=None):
        slots = w_get_batch(4)
        st_fn, ap_fn = after_block if after_block is not None else (None, None)
        for b, (c0, n) in enumerate(BLOCKS):
            for m in range(KC):
                bk = pool_a.next()
                pairs = [(wsl[:, slots[kk], m * 128:(m + 1) * 128], y_sb[:, kk, c0:c0 + n]) for kk in range(4)]
                mm_group(bk, n, pairs, reads=[w_r[s] for s in slots] + [y_r[kk][b] for kk in range(4)])
                P.op("dve", (lambda e, bk=bk, m=m, c0=c0, n=n: e.tensor_tensor(
                    out=x_sb[:, m, c0:c0 + n], in0=x_sb[:, m, c0:c0 + n], in1=banks[bk][:, 0:n], op=ALU.add)),
                    reads=[bank_r[bk], x_r[m][b]], writes=[x_r[m][b]])
                if b * KC + m >= DRAIN_DELAY:
                    after_major()
                if ap_fn is not None and m == 3 and b >= 2:
                    ap_fn(b - 2)
            if st_fn is not None and b >= 1:
                st_fn(b - 1)
        if st_fn is not None:
            ap_fn(NB - 2)
            st_fn(NB - 1)
            ap_fn(NB - 1)
        if DEBUG:
            flush_deferred()
            for kk in range(4):
                P.op("sp", (lambda e, kk=kk: e.dma_start(out=dbg_y[:, l, 4 * rnd + kk, :], in_=y_sb[:, kk, SEQ:T])),
                     reads=[y_r[kk][NB - 1]], chan=ch_dbg)
            if rnd == 3:
                P.op("sp", (lambda e: e.dma_start(out=dbg_x[:, l], in_=x_sb[:, :, SEQ:T])),
                     reads=[x_r[m][NB - 1] for m in range(KC)], chan=ch_dbg)

    def a_head_prep(l, j):
        s = j % 2
        P.op("dve", (lambda e, s=s, l=l, j=j: e.tensor_tensor(
            out=mk[:, s, N_DVE:WA, :], in0=cmat[:, 0, :].unsqueeze(1).broadcast_to([128, WA - N_DVE, 128]),
            in1=cw_a[:, l, j, N_DVE:WA].unsqueeze(2).broadcast_to([128, WA - N_DVE, 128]), op=ALU.mult)),
            reads=[const_r], writes=[mk_r[s]])

    def a_hist(l, j):
        P.op("sp", (lambda e, l=l, j=j: e.dma_start(out=hista[:], in_=hist_a_d[:, l, j])), writes=[hista_r], chan=ch_hista)
        P.op("dve", (lambda e: e.tensor_scalar_mul(out=ubuf[:, US:US + (WA - 1) * NS], in0=hista[:], scalar1=2.0)),
             reads=[hista_r], writes=[uh_r])

    def a_step(l, j, b, jj, slots):
        c0, n = BLOCKS[b]
        sg, sv, sz_w = slots
        hreads = [h_r[k][b] for k in range(KC)]
        bg = pool_a.next()
        mm_group(bg, n, [(wsl[:, sg, k * 128:(k + 1) * 128], h_sb[:, k, c0:c0 + n]) for k in range(KC)], reads=[w_r[sg]] + hreads)
        pop_rank(1)
        ith = rot_th.next()
        P.op("act", (lambda e, bg=bg, ith=ith, n=n: e.activation(out=th_t[:, ith, 0:n], in_=banks[bg][:, 0:n], func=AF.Tanh, scale=0.5)),
             reads=[bank_r[bg]], writes=[th_r[ith]], name=f"th{j}.{b}")
        bv = pool_a.next()
        mm_group(bv, n, [(wsl[:, sv, k * 128:(k + 1) * 128], h_sb[:, k, c0:c0 + n]) for k in range(KC)], reads=[w_r[sv]] + hreads)
        if b < NB - 1:
            uout = ubuf[:, UH + c0:UH + c0 + n]
        else:
            uout = ubuf[:, US + (WA - 1) * NS:UW]
        P.op("dve", (lambda e, bv=bv, ith=ith, n=n, uout=uout, b=b: e.scalar_tensor_tensor(
            out=uout, in0=th_t[:, ith, 0:n], scalar=1.0, in1=banks[bv][:, 0:n],
            op0=ALU.add, op1=ALU.mult)),
            reads=[bank_r[bv], th_r[ith]], writes=[u_r[b]])
        if b == NB - 2:
            P.op("dve", (lambda e, bv=bv, ith=ith, n=n, j=j: e.scalar_tensor_tensor(
                out=oa_t[:, j, 0:WA - 1], in0=th_t[:, ith, n - (WA - 1):n], scalar=1.0, in1=banks[bv][:, n - (WA - 1):n],
                op0=ALU.add, op1=ALU.mult)),
                reads=[bank_r[bv], th_r[ith]], writes=[oa_r[j]])
        if b == NB - 1:
            P.op("dve", (lambda e, bv=bv, ith=ith, n=n, j=j: e.scalar_tensor_tensor(
                out=oa_t[:, j, WA - 1:WA - 1 + n], in0=th_t[:, ith, 0:n], scalar=1.0, in1=banks[bv][:, 0:n],
                op0=ALU.add, op1=ALU.mult)),
                reads=[bank_r[bv], th_r[ith]], writes=[oa_r[j]])
        pop_rank(2)
        bz = pool_a.next()
        mm_group(bz, n, [(wsl[:, sz_w, k * 128:(k + 1) * 128], h_sb[:, k, c0:c0 + n]) for k in range(KC)], reads=[w_r[sz_w]] + hreads)
        pop_rank(3)
        isz = rot_sz.next()
        P.op("act", (lambda e, bz=bz, isz=isz, n=n: e.activation(out=sz_t[:, isz, 0:n], in_=banks[bz][:, 0:n], func=AF.Silu)),
             reads=[bank_r[bz]], writes=[sz_r[isz]], name=f"sz{j}.{b}")
        pop_rank(0)
        bd = pool_d.next()
        ms = j % 2
        if b < NB - 1:
            def usl(k):
                return ubuf[:, c0 + k:c0 + k + n]
            creads = [u_r[b]] + ([u_r[b - 1]] if b > 0 else [pad_r])
        else:
            def usl(k):
                return ubuf[:, US + NS * k:US + NS * k + n]
            creads = [u_r[b], uh_r]
        nd = N_DVE_PRE_SAMPLE if b == NB - 2 else N_DVE
        pe_taps = list(range(nd, WA))

        def pe_fn(e):
            ins = None
            for i, k in enumerate(pe_taps):
                ins = e.matmul(banks[bd][:, 0:n], lhsT=mk[:, ms, k, :], rhs=usl(k), start=(i == 0), stop=(nd == 0 and i == len(pe_taps) - 1))
            return ins
        P.op("pe", pe_fn, reads=[mk_r[ms]] + creads, writes=[bank_r[bd]])
        iac = rot_acc.next()

        def dve_taps(k0, k1):
            chains = [[k for k in range(k0, k1) if k % 2 == 0], [k for k in range(k0, k1) if k % 2 == 1]]
            order = []
            for i in range(max(len(chains[0]), len(chains[1]))):
                for c in range(2):
                    if i < len(chains[c]):
                        order.append((c, i, chains[c][i]))
            for c, i, k in order:
                wcol = cwh[:, l, j, k:k + 1]
                last = (i == len(chains[c]) - 1)
                outap = accb[:, 2 * iac + c, 0:n] if last else acc[:, c, 0:n]
                wres = accb_r[2 * iac + c] if last else acc_r[c]
                if i == 0:
                    P.op("dve", (lambda e, k=k, wcol=wcol, outap=outap: e.tensor_scalar_mul(out=outap, in0=usl(k), scalar1=wcol)),
                         reads=creads + [cwh_r], writes=[wres])
                else:
                    P.op("dve", (lambda e, k=k, wcol=wcol, outap=outap, c=c: e.scalar_tensor_tensor(
                        out=outap, in0=usl(k), scalar=wcol, in1=acc[:, c, 0:n], op0=ALU.mult, op1=ALU.add)),
                        reads=creads + [cwh_r, acc_r[c]], writes=[wres])
            return [c for c in range(2) if chains[c]]
        used_chains = dve_taps(0, nd)
        pop_rank(4)
        id2 = rot_d2.next()
        cbc = cbias[:, l * 8 + j:l * 8 + j + 1]
        bvar_box = {}

        def stage0():
            for ci, c in enumerate(used_chains):
                P.op("pe", (lambda e, c=c, ci=ci: e.matmul(banks[bd][:, 0:n], lhsT=cb16[:], rhs=accb[:, 2 * iac + c, 0:n],
                                                        start=False, stop=(ci == len(used_chains) - 1))),
                     reads=[accb_r[2 * iac + c], cb16_r], writes=[bank_r[bd]])
            P.op("act", (lambda e: e.activation(out=d2[:, id2, 0:n], in_=banks[bd][:, 0:n], func=AF.Square, bias=cbc)),
                 reads=[bank_r[bd], cbias_r], writes=[d2_r[id2]], name=f"d2_{j}.{b}")

        def stage1():
            bvv = pool_a.next()
            bvar_box["b"] = bvv
            mm_group(bvv, n, [(ones_v[:], d2[:, id2, 0:n])], reads=[d2_r[id2], ones_r])

        def stage2():
            bvv = bvar_box["b"]
            P.op("act", (lambda e: e.activation(out=lnv_t[:, 0, 0:n], in_=banks[bvv][:, 0:n], func=AF.Ln, bias=LN_EPS)),
                 reads=[bank_r[bvv]], writes=[lnv_r], name=f"ln{j}.{b}")
            P.op("act", (lambda e: e.activation(out=rstd_t[:, 0, 0:n], in_=lnv_t[:, 0, 0:n], func=AF.Exp, scale=-0.5)),
                 reads=[lnv_r], writes=[rstd_r], name=f"exp{j}.{b}")
            P.op("dve", (lambda e: e.scalar_tensor_tensor(out=tt_t[:, 0, 0:n], in0=banks[bd][:, 0:n], scalar=cbc, in1=rstd_t[:, 0, 0:n],
                                                            op0=ALU.add, op1=ALU.mult)),
                 reads=[bank_r[bd], rstd_r, cbias_r], writes=[tt_r])

        def stage3():
            gcol = vecs[:, 32 + l * 8 + j:32 + l * 8 + j + 1]
            bcol = vecs[:, 48 + l * 8 + j:48 + l * 8 + j + 1]
            P.op("act", (lambda e: e.activation(out=ss_t[:, 0, 0:n], in_=tt_t[:, 0, 0:n], func=AF.Silu, scale=gcol, bias=bcol)),
                 reads=[tt_r, const_r], writes=[ss_r], name=f"s{j}.{b}")

        def stage4():
            P.op("dve", (lambda e: e.tensor_tensor(out=y_sb[:, jj, c0:c0 + n], in0=ss_t[:, 0, 0:n], in1=sz_t[:, isz, 0:n], op=ALU.mult)),
                 reads=[ss_r, sz_r[isz]], writes=[y_r[jj][b]])
        sid = step_ctr[0]
        step_ctr[0] += 1
        for rk, fn in enumerate([stage0, stage1, stage2, stage3, stage4]):
            defer(sid, rk, fn)

    def a_finish(l):
        P.op("dve", (lambda e: e.tensor_scalar_mul(out=oa_t[:], in0=oa_t[:], scalar1=0.5)),
             reads=oa_r, writes=oa_r)
        P.op("sp", (lambda e, l=l: e.dma_start(out=oa_p[:, l], in_=oa_t[:, :, 0:WA - 1])), reads=oa_r, chan=ch_oa)
        P.op("sp", (lambda e, l=l: e.dma_start(out=oa_new[:, l], in_=oa_t[:, :, WA - 1:WA - 1 + NS * DS])), reads=oa_r, chan=ch_oa)

    def b_group_prep(l, j):
        s = j % 2
        P.op("dve", (lambda e, s=s, l=l, j=j: e.tensor_tensor(
            out=dk[:, s, :, :], in0=cmat[:, 1, :].unsqueeze(1).broadcast_to([128, WB, 128]),
            in1=cw_b[:, l, j, :].unsqueeze(2).broadcast_to([128, WB, 128]), op=ALU.mult)),
            reads=[const_r], writes=[dk_r[s]])

    def b_hist(l, j):
        P.op("dve", (lambda e, l=l, j=j: e.tensor_copy(vbuf[:, VS:VS + (WB - 1) * NS], histb[:, l, j, :])),
             reads=[histb_r], writes=[vh_r])

    def b_step(l, j, b, jj, slots):
        c0, n = BLOCKS[b]
        s_zb, s_gb, s_gc, s_hb = slots
        hreads = [h_r[k][b] for k in range(KC)]

        def inproj(slot):
            bk = pool_a.next()
            mm_group(bk, n, [(wsl[:, slot, k * 128:(k + 1) * 128], h_sb[:, k, c0:c0 + n]) for k in range(KC)], reads=[w_r[slot]] + hreads)
            after_major()
            return bk
        bzb = inproj(s_zb)
        if b == 0:
            b_hist(l, j)
        isz = rot_sz.next()
        P.op("act", (lambda e: e.activation(out=sz_t[:, isz, 0:n], in_=banks[bzb][:, 0:n], func=AF.Silu)),
             reads=[bank_r[bzb]], writes=[sz_r[isz]])
        bgb = inproj(s_gb)
        igz = rot_gz.next()
        P.op("dve", (lambda e: e.tensor_tensor(out=gz_t[:, igz, 0:n], in0=banks[bgb][:, 0:n], in1=sz_t[:, isz, 0:n], op=ALU.mult)),
             reads=[bank_r[bgb], sz_r[isz]], writes=[gz_r[igz]])
        bgc = inproj(s_gc)
        P.op("act", (lambda e: e.activation(out=gcs_t[:, 0, 0:n], in_=banks[bgc][:, 0:n], func=AF.Copy)),
             reads=[bank_r[bgc]], writes=[gcs_r])
        bhb = inproj(s_hb)
        if b < NB - 1:
            vout = vbuf[:, VH + c0:VH + c0 + n]
        else:
            vout = vbuf[:, VS + (WB - 1) * NS:VW]
        P.op("dve", (lambda e: e.tensor_tensor(out=vout, in0=banks[bhb][:, 0:n], in1=gcs_t[:, 0, 0:n], op=ALU.mult)),
             reads=[bank_r[bhb], gcs_r], writes=[v_r[b]])
        if b == NB - 2:
            P.op("dve", (lambda e: e.tensor_tensor(out=ob_t[:, j, 0:WB - 1], in0=banks[bhb][:, n - (WB - 1):n], in1=gcs_t[:, 0, n - (WB - 1):n], op=ALU.mult)),
                 reads=[bank_r[bhb], gcs_r], writes=[ob_r[j]])
        if b == NB - 1:
            P.op("dve", (lambda e: e.tensor_tensor(out=ob_t[:, j, WB - 1:WB - 1 + n], in0=banks[bhb][:, 0:n], in1=gcs_t[:, 0, 0:n], op=ALU.mult)),
                 reads=[bank_r[bhb], gcs_r], writes=[ob_r[j]])

        if b < NB - 1:
            def vsl(k):
                return vbuf[:, c0 + k:c0 + k + n]
            creads = [v_r[b]] + ([v_r[b - 1]] if b > 0 else [pad_r])
        else:
            def vsl(k):
                return vbuf[:, VS + NS * k:VS + NS * k + n]
            creads = [v_r[b], vh_r]
        iac = rot_acc.next()
        for k in range(WB):
            wcol = cw_b[:, l, j, k:k + 1]
            if k == 0:
                P.op("dve", (lambda e, k=k, wcol=wcol: e.tensor_scalar_mul(out=acc[:, iac, 0:n], in0=vsl(k), scalar1=wcol)),
                     reads=creads + [const_r], writes=[acc_r[iac]], relaxed=(acc_r[iac],))
            else:
                P.op("dve", (lambda e, k=k, wcol=wcol: e.scalar_tensor_tensor(
                    out=acc[:, iac, 0:n], in0=vsl(k), scalar=wcol, in1=acc[:, iac, 0:n], op0=ALU.mult, op1=ALU.add)),
                    reads=creads + [const_r, acc_r[iac]], writes=[acc_r[iac]], relaxed=(acc_r[iac],))
        P.op("dve", (lambda e: e.tensor_tensor(out=y_sb[:, jj, c0:c0 + n], in0=acc[:, iac, 0:n], in1=gz_t[:, igz, 0:n], op=ALU.mult)),
             reads=[acc_r[iac], gz_r[igz]], writes=[y_r[jj][b]], relaxed=(acc_r[iac],))

    def b_finish(l):
        P.op("sp", (lambda e, l=l: e.dma_start(out=ob_p[:, l], in_=ob_t[:, :, 0:WB - 1])), reads=ob_r, chan=ch_ob)
        P.op("sp", (lambda e, l=l: e.dma_start(out=ob_new[:, l], in_=ob_t[:, :, WB - 1:WB - 1 + NS * DS])), reads=ob_r, chan=ch_ob)

    w_prefetch(NW)
    for l in range(DEPTH):
        if l == 0:
            rms_phase(0)
            a_head_prep(l, 0)
        for half in range(2):
            for j in range(4 * half, 4 * half + 4):
                slots = w_get_batch(3)
                if j + 1 < 8:
                    a_head_prep(l, j + 1)
                a_hist(l, j)
                for b in range(NB):
                    a_step(l, j, b, j - 4 * half, slots)
            out_round(l, half)
            flush_deferred()
        a_finish(l)
        for half in range(2):
            for j in range(4 * half, 4 * half + 4):
                slots = w_get_batch(4)
                for b in range(NB):
                    b_step(l, j, b, j - 4 * half, slots)
            nxt = None
            if half == 1 and l + 1 < DEPTH:
                a_head_prep(l + 1, 0)
            if half == 1:
                if l + 1 < DEPTH:
                    nxt = ((lambda b: rms_stats(b, pops=False)), (lambda b, l=l: rms_apply((l + 1) * 8, b)))
                else:
                    nxt = ((lambda b: rms_stats(b, pops=False)), (lambda b: rms_apply(64, b, final=True)))
            out_round(l, 2 + half, after_block=nxt)
            flush_deferred()
        b_finish(l)

    final_waits = ch_out + [ch_oa, ch_ob, ch_copy] + ([ch_dbg] if DEBUG else [])
    sem_ctx = {}
    for e in Prog.ENGS:
        sem_ctx[e] = es.enter_context(nc.semaphore(f"s_{e}"))
    chan_sems = {c: es.enter_context(nc.semaphore(f"c_{c}")) for c in chans}
    block = es.enter_context(nc.Block())
    P.emit(nc, block, sem_ctx, chan_sems, final_waits)
    es.close()
    return nc


_NC_CACHE = {}


def _get_nc():
    if "nc" not in _NC_CACHE:
        _NC_CACHE["nc"] = build_nc()
    return _NC_CACHE["nc"]


def _prep_shared(norm_g, w_in, conv_a_w, conv_a_b, ln_a_g, ln_a_b, conv_b_w, w_out, final_g):
    f = np.float32
    w_in_r = np.ascontiguousarray(
        np.asarray(w_in, f).reshape(DEPTH, KC, 128, 56, 128).transpose(0, 3, 2, 1, 4)).reshape(DEPTH, 56, 128, 1024)
    w_out_r = np.ascontiguousarray(np.asarray(w_out, f).reshape(DEPTH, 16, 128, 1024))
    cw_a = np.ascontiguousarray(np.asarray(conv_a_w, f).reshape(DEPTH, WA, 8, 128).transpose(3, 0, 2, 1))
    cw_b = np.ascontiguousarray(np.asarray(conv_b_w, f).reshape(DEPTH, WB, 8, 128).transpose(3, 0, 2, 1))

    def pv(v):
        return np.asarray(v, f).reshape(-1, 8, 128).transpose(2, 0, 1).reshape(128, -1)
    vecs = np.ascontiguousarray(np.concatenate(
        [pv(norm_g), pv(conv_a_b), pv(ln_a_g), pv(ln_a_b), pv(np.asarray(final_g, f)[None])], axis=1))
    cmat = np.zeros((128, 2, 128), f)
    cmat[:, 0, :] = 0.5 * (np.eye(128, dtype=f) - f(1.0 / 128))
    cmat[:, 1, :] = np.eye(128, dtype=f)
    return dict(w_in_r=w_in_r, w_out_r=w_out_r, cw_a=cw_a, cw_b=cw_b, vecs=vecs, cmat=cmat)


def kernel(x_prompt, x_sample, state_conv_a, state_conv_b, norm_g, w_in, conv_a_w, conv_a_b,
           ln_a_g, ln_a_b, conv_b_w, w_out, final_g):
    f = np.float32
    nc = _get_nc()
    shared = _prep_shared(norm_g, w_in, conv_a_w, conv_a_b, ln_a_g, ln_a_b, conv_b_w, w_out, final_g)
    x_prompt = np.asarray(x_prompt, f)
    x_sample = np.asarray(x_sample, f)
    state_conv_a = np.asarray(state_conv_a, f)
    state_conv_b = np.asarray(state_conv_b, f)
    in_maps = []
    for c in range(N_CORES):
        xs = x_sample[NS * c:NS * (c + 1)].transpose(1, 0, 2).reshape(NS * DS, D)
        xt = np.concatenate([x_prompt[c], xs], axis=0)
        xT = np.ascontiguousarray(xt.reshape(T, KC, 128).transpose(2, 1, 0))
        sa = state_conv_a[:, NS * c:NS * (c + 1)]
        sbb = state_conv_b[:, NS * c:NS * (c + 1)]
        hist_a = np.ascontiguousarray(sa.reshape(DEPTH, NS, WA - 1, 8, 128).transpose(4, 0, 3, 2, 1)).reshape(128, DEPTH, 8, (WA - 1) * NS)
        hist_b = np.ascontiguousarray(sbb.reshape(DEPTH, NS, WB - 1, 8, 128).transpose(4, 0, 3, 2, 1)).reshape(128, DEPTH, 8, (WB - 1) * NS)
        m = dict(shared)
        m.update(xT=xT, hist_a=hist_a, hist_b=hist_b, sa_orig=np.ascontiguousarray(sa))
        in_maps.append(m)
    res = run_bass_kernel_spmd(nc, in_maps, core_ids=list(range(N_CORES)))
    y_prompt = np.empty((N_CORES, SEQ, D), f)
    y_sample = np.empty((N_CORES * NS, DS, D), f)
    na_p = np.empty((DEPTH, N_CORES, WA - 1, D), f)
    nb_p = np.empty((DEPTH, N_CORES, WB - 1, D), f)
    na_s = np.empty((DEPTH, N_CORES * NS, WA - 1, D), f)
    nb_s = np.empty((DEPTH, N_CORES * NS, WB - 1, D), f)
    for c in range(N_CORES):
        r = res.results[c]
        yfm = np.asarray(r["y_fm"], f)
        ytm = yfm.transpose(2, 1, 0).reshape(T, D)
        y_prompt[c] = ytm[:SEQ]
        y_sample[NS * c:NS * (c + 1)] = ytm[SEQ:].reshape(DS, NS, D).transpose(1, 0, 2)
        oap = np.asarray(r["oa_p"], f)
        na_p[:, c] = oap.transpose(1, 3, 2, 0).reshape(DEPTH, WA - 1, D)
        obp = np.asarray(r["ob_p"], f)
        nb_p[:, c] = obp.transpose(1, 3, 2, 0).reshape(DEPTH, WB - 1, D)
        na_s[:, NS * c:NS * (c + 1), :WA - 1 - DS] = np.asarray(r["oa_copy"], f)
        oan = np.asarray(r["oa_new"], f).reshape(128, DEPTH, 8, DS, NS)
        na_s[:, NS * c:NS * (c + 1), WA - 1 - DS:] = oan.transpose(1, 4, 3, 2, 0).reshape(DEPTH, NS, DS, D)
        obn = np.asarray(r["ob_new"], f).reshape(128, DEPTH, 8, DS, NS)
        nb_s[:, NS * c:NS * (c + 1)] = obn.transpose(1, 4, 3, 2, 0).reshape(DEPTH, NS, DS, D)[:, :, DS - (WB - 1):]
    if DEBUG:
        _NC_CACHE["dbg"] = [(np.asarray(r["dbg_y"]).astype(f), np.asarray(r["dbg_x"], f)) for r in res.results]
    return (y_prompt, y_sample, na_p, nb_p, na_s, nb_s)
```
